# Optimizing a Trainium2 kernel written in Bass

```python
import math
import jax, jax.numpy as jnp
from jax import lax
import numpy as np

D_MODEL = 2048
BATCH = 4
SEQ = 2048
DEPTH = 1

MOBA_HEADS = 8
HEAD_DIM = 128
MOBA_BLOCK = 256
MOBA_TOPK = 3
MLA_HEADS = 8
MLA_Q_LORA = 512
MLA_KV_LORA = 256
MLA_NOPE = 128
MLA_ROPE = 64
MLA_V = 128
ROPE_THETA = 10000.0
MEM_TOKENS = 256
MEM_HEADS = 4
MEM_HEAD_DIM = 128
N_BUCKETS = 32
MAX_DISTANCE = 128
Q_BLOCK = 128
EPS = 1e-6

MOBA_W = MOBA_HEADS * HEAD_DIM
MLA_W = MLA_HEADS * MLA_V
MEM_W = MEM_HEADS * MEM_HEAD_DIM
IN_SPLITS = (MOBA_W, MOBA_W, MOBA_W, MOBA_W,
             MLA_Q_LORA, MLA_KV_LORA, MLA_ROPE, MLA_W,
             MEM_W, MEM_W,
             D_MODEL, D_MODEL, D_MODEL)
IN_WIDTH = sum(IN_SPLITS)

kernel_name = "hybrid_moba_mla_memory_gated_block"


def rms_norm(x, g):
    xf = x.astype(jnp.float32)
    y = xf * lax.rsqrt(jnp.mean(xf * xf, axis=-1, keepdims=True) + EPS)
    return (y * g.astype(jnp.float32)).astype(x.dtype)


def rope(x, pos):
    half = x.shape[-1] // 2
    inv = ROPE_THETA ** (-jnp.arange(half, dtype=jnp.float32) / half)
    ang = pos.astype(jnp.float32)[:, None] * inv[None, :]
    cos = jnp.cos(ang)[None, :, None, :]
    sin = jnp.sin(ang)[None, :, None, :]
    x1, x2 = x[..., :half], x[..., half:]
    return jnp.concatenate([x1 * cos - x2 * sin, x2 * cos + x1 * sin], axis=-1).astype(x.dtype)


def t5_bucket(dist):
    n = jnp.maximum(dist, 0)
    max_exact = N_BUCKETS // 2
    nf = jnp.maximum(n, 1).astype(jnp.float32)
    large = max_exact + (jnp.log(nf / max_exact) / math.log(MAX_DISTANCE / max_exact)
                         * (N_BUCKETS - max_exact)).astype(jnp.int32)
    large = jnp.minimum(large, N_BUCKETS - 1)
    return jnp.where(n < max_exact, n, large)


def moba_attention(q, k, v, rel_bias):
    B, S, H, dh = q.shape
    nb = -(-S // MOBA_BLOCK)
    s_pad = nb * MOBA_BLOCK
    pad = ((0, 0), (0, s_pad - S), (0, 0), (0, 0))
    kb = jnp.pad(k, pad).reshape(B, nb, MOBA_BLOCK, H, dh).transpose(0, 3, 1, 2, 4)
    vb = jnp.pad(v, pad).reshape(B, nb, MOBA_BLOCK, H, dh).transpose(0, 3, 1, 2, 4)
    k_mean = jnp.mean(kb.astype(jnp.float32), axis=3)
    qh = q.transpose(0, 2, 1, 3)
    q_blk = jnp.arange(S, dtype=jnp.int32) // MOBA_BLOCK
    gate = jnp.einsum('bhsd,bhnd->bhsn', qh.astype(jnp.float32), k_mean)
    past = jnp.arange(nb, dtype=jnp.int32)[None, :] < q_blk[:, None]
    gate = jnp.where(past[None, None], gate, -jnp.inf)
    ksel = min(MOBA_TOPK, nb)
    _, sel = lax.top_k(gate, ksel)
    sel = sel.astype(jnp.int32)
    sel_valid = sel < q_blk[None, None, :, None]

    nc = S // Q_BLOCK
    n_items = B * H * nc
    q_items = qh.reshape(n_items, Q_BLOCK, dh)
    sel_items = sel.reshape(n_items, Q_BLOCK, ksel)
    valid_items = sel_valid.reshape(n_items, Q_BLOCK, ksel)
    ids = jnp.arange(n_items, dtype=jnp.int32)
    bh_ids = ids // nc
    c_ids = ids % nc
    h_ids = bh_ids % H
    kb_flat = kb.reshape(B * H, nb, MOBA_BLOCK, dh)
    vb_flat = vb.reshape(B * H, nb, MOBA_BLOCK, dh)
    scale = HEAD_DIM ** -0.5
    blk_off = jnp.arange(MOBA_BLOCK, dtype=jnp.int32)

    def one(args):
        qc, selc, validc, bh, c, hh = args
        kbh = kb_flat[bh]
        vbh = vb_flat[bh]
        qpos = c * Q_BLOCK + jnp.arange(Q_BLOCK, dtype=jnp.int32)
        own = (c * Q_BLOCK) // MOBA_BLOCK
        k_own, v_own = kbh[own], vbh[own]
        k_sel, v_sel = kbh[selc], vbh[selc]
        l_sel = jnp.einsum('qd,qjkd->qjk', qc, k_sel).reshape(Q_BLOCK, ksel * MOBA_BLOCK)
        l_own = jnp.einsum('qd,kd->qk', qc, k_own)
        kpos_sel = (selc[:, :, None] * MOBA_BLOCK + blk_off).reshape(Q_BLOCK, ksel * MOBA_BLOCK)
        kpos_own = own * MOBA_BLOCK + blk_off
        kpos = jnp.concatenate([kpos_sel, jnp.broadcast_to(kpos_own, (Q_BLOCK, MOBA_BLOCK))], axis=-1)
        bias = rel_bias[t5_bucket(qpos[:, None] - kpos), hh].astype(jnp.float32)
        logits = jnp.concatenate([l_sel, l_own], axis=-1).astype(jnp.float32) * scale + bias
        m_sel = jnp.broadcast_to(validc[:, :, None], (Q_BLOCK, ksel, MOBA_BLOCK)).reshape(Q_BLOCK, ksel * MOBA_BLOCK)
        m_own = kpos_own[None, :] <= qpos[:, None]
        mask = jnp.concatenate([m_sel, m_own], axis=-1)
        p = jax.nn.softmax(jnp.where(mask, logits, -jnp.inf), axis=-1).astype(v.dtype)
        p_sel, p_own = p[:, :ksel * MOBA_BLOCK], p[:, ksel * MOBA_BLOCK:]
        return (jnp.einsum('qk,qkd->qd', p_sel, v_sel.reshape(Q_BLOCK, ksel * MOBA_BLOCK, dh))
                + jnp.einsum('qk,kd->qd', p_own, v_own))

    out = lax.map(one, (q_items, sel_items, valid_items, bh_ids, c_ids, h_ids))
    return out.reshape(B, H, S, dh).transpose(0, 2, 1, 3).reshape(B, S, H * dh)


def causal_attention(q, k, v):
    B, S, H, dk = q.shape
    dv = v.shape[-1]
    nc = S // Q_BLOCK
    qb = q.reshape(B, nc, Q_BLOCK, H, dk).transpose(1, 0, 2, 3, 4)
    kpos = jnp.arange(S, dtype=jnp.int32)
    scale = dk ** -0.5

    def blk(args):
        qc, c = args
        qpos = c * Q_BLOCK + jnp.arange(Q_BLOCK, dtype=jnp.int32)
        logits = jnp.einsum('bqhd,bkhd->bhqk', qc, k).astype(jnp.float32) * scale
        logits = jnp.where(kpos[None, :] <= qpos[:, None], logits, -jnp.inf)
        p = jax.nn.softmax(logits, axis=-1).astype(v.dtype)
        return jnp.einsum('bhqk,bkhd->bqhd', p, v)

    out = lax.map(blk, (qb, jnp.arange(nc, dtype=jnp.int32)))
    return out.transpose(1, 0, 2, 3, 4).reshape(B, S, H * dv)


def memory_attention(q, k, v):
    B, S, H, d = q.shape
    logits = jnp.einsum('bshd,bmhd->bhsm', q, k).astype(jnp.float32) * (d ** -0.5)
    p = jax.nn.softmax(logits, axis=-1).astype(v.dtype)
    return jnp.einsum('bhsm,bmhd->bshd', p, v).reshape(B, S, H * d)


def setup_inputs(seed: int = 0) -> dict:
    key = jax.random.key(seed)
    ks = jax.random.split(key, 18)
    f32 = jnp.float32

    def nrm(k, shape, fan_in):
        return jax.random.normal(k, shape, f32) * fan_in ** -0.5

    def gain(k, shape):
        return 1.0 + 0.1 * jax.random.normal(k, shape, f32)

    L = DEPTH
    return {
        "x": jax.random.normal(ks[0], (BATCH, SEQ, D_MODEL), f32),
        "mem": jax.random.normal(ks[1], (BATCH, MEM_TOKENS, D_MODEL), f32),
        "g_norm": gain(ks[2], (L, D_MODEL)),
        "w_in": nrm(ks[3], (L, D_MODEL, IN_WIDTH), D_MODEL),
        "g_cq": gain(ks[4], (L, MLA_Q_LORA)),
        "w_uq": nrm(ks[5], (L, MLA_Q_LORA, MLA_HEADS * (MLA_NOPE + MLA_ROPE)), MLA_Q_LORA),
        "g_ckv": gain(ks[6], (L, MLA_KV_LORA)),
        "w_ukv": nrm(ks[7], (L, MLA_KV_LORA, MLA_HEADS * (MLA_NOPE + MLA_V)), MLA_KV_LORA),
        "g_mem": gain(ks[8], (L, D_MODEL)),
        "w_mem_kv": nrm(ks[9], (L, D_MODEL, 2 * MEM_W), D_MODEL),
        "rel_bias": 0.5 * jax.random.normal(ks[10], (N_BUCKETS, MOBA_HEADS), f32),
        "w_p_moba": nrm(ks[11], (L, MOBA_W, D_MODEL), MOBA_W),
        "w_p_mla": nrm(ks[12], (L, MLA_W, D_MODEL), MLA_W),
        "w_p_mem": nrm(ks[13], (L, MEM_W, D_MODEL), MEM_W),
        "w_out": nrm(ks[14], (L, D_MODEL, D_MODEL), D_MODEL),
        "g_final": gain(ks[15], (D_MODEL,)),
    }


def reference(x, mem, g_norm, w_in, g_cq, w_uq, g_ckv, w_ukv, g_mem, w_mem_kv, rel_bias,
              w_p_moba, w_p_mla, w_p_mem, w_out, g_final):
    B, S, _ = x.shape
    M = mem.shape[1]
    pos = jnp.arange(S, dtype=jnp.int32)
    offsets = [int(o) for o in np.cumsum(IN_SPLITS)[:-1]]
    for l in range(DEPTH):
        h = rms_norm(x, g_norm[l])
        u = h @ w_in[l]
        (q_a, k_a, v_a, z_a, c_q, c_kv, k_r, z_b, q_m, z_m,
         gl_a, gl_b, gl_m) = jnp.split(u, offsets, axis=-1)

        o_a = moba_attention(q_a.reshape(B, S, MOBA_HEADS, HEAD_DIM),
                             k_a.reshape(B, S, MOBA_HEADS, HEAD_DIM),
                             v_a.reshape(B, S, MOBA_HEADS, HEAD_DIM), rel_bias)

        q_b = (rms_norm(c_q, g_cq[l]) @ w_uq[l]).reshape(B, S, MLA_HEADS, MLA_NOPE + MLA_ROPE)
        q_b = jnp.concatenate([q_b[..., :MLA_NOPE], rope(q_b[..., MLA_NOPE:], pos)], axis=-1)
        kv_b = (rms_norm(c_kv, g_ckv[l]) @ w_ukv[l]).reshape(B, S, MLA_HEADS, MLA_NOPE + MLA_V)
        k_rope = jnp.broadcast_to(rope(k_r[:, :, None, :], pos), (B, S, MLA_HEADS, MLA_ROPE))
        k_b = jnp.concatenate([kv_b[..., :MLA_NOPE], k_rope], axis=-1)
        o_b = causal_attention(q_b, k_b, kv_b[..., MLA_NOPE:])

        mkv = rms_norm(mem, g_mem[l]) @ w_mem_kv[l]
        k_m = mkv[..., :MEM_W].reshape(B, M, MEM_HEADS, MEM_HEAD_DIM)
        v_m = mkv[..., MEM_W:].reshape(B, M, MEM_HEADS, MEM_HEAD_DIM)
        o_m = memory_attention(q_m.reshape(B, S, MEM_HEADS, MEM_HEAD_DIM), k_m, v_m)

        p_a = (o_a * jax.nn.silu(z_a)) @ w_p_moba[l]
        p_b = (o_b * jax.nn.silu(z_b)) @ w_p_mla[l]
        p_m = (o_m * jax.nn.silu(z_m)) @ w_p_mem[l]
        y = jax.nn.sigmoid(gl_a) * p_a + jax.nn.sigmoid(gl_b) * p_b + jax.nn.sigmoid(gl_m) * p_m
        x = x + y @ w_out[l]
    return rms_norm(x, g_final)
```

```python
import contextlib
import math
import numpy as np
import ml_dtypes
import concourse.bass as bass
import concourse.mybir as mybir
from concourse.bass_utils import run_bass_kernel_spmd

F32 = mybir.dt.float32
BF16 = mybir.dt.bfloat16
AF = mybir.ActivationFunctionType
ALU = mybir.AluOpType
AX = mybir.AxisListType

D = 2048
S = 2048
B = 4
NCORE = 8
T_OWN = 1024
EPS = 1e-6
NEG = -30000.0
SC_MOBA = 128 ** -0.5
SC_MLA = 192 ** -0.5
SC_MEM = 128 ** -0.5

CH_MOBA = 0
CH_CKV = 32
CH_KR = 34
CH_CQ = 35
CH_ZB = 39
CH_QM = 47
CH_ZM = 51
CH_GL = 55
N_WIN = 103
STOP = 99
NH1 = 8
DBG = 99
LDBG = 99
DUMP = False


class Op:
    __slots__ = ("eng", "fn", "deps", "dsem", "signal", "count")

    def __init__(self, eng, fn, deps, dsem):
        self.eng = eng
        self.fn = fn
        self.deps = deps
        self.dsem = dsem
        self.signal = False
        self.count = 0


class Sched:
    ENGS = ("pe", "act", "dve", "pool", "sp")

    def __init__(self, nc, es):
        self.nc = nc
        self.es = es
        self.eng = dict(pe=nc.tensor, act=nc.scalar, dve=nc.vector, pool=nc.gpsimd, sp=nc.sync)
        self.ops = []
        self.lw = {}
        self.rd = {}
        self.last_on_eng = {}
        self.dma_pending = []

    def add(self, eng, fn, reads=(), writes=(), dsem=None):
        i = len(self.ops)
        deps = set()
        for k in reads:
            w = self.lw.get(k)
            if w is not None:
                deps.add(w)
            if k[0] == "ps":
                r = self.rd.get(k)
                if r:
                    deps.update(v for e_, v in r[0].items() if e_ != eng)
        for k in writes:
            w = self.lw.get(k)
            if w is not None:
                deps.add(w)
            r = self.rd.get(k)
            if r:
                deps.update(r[0].values())
                deps.update(r[1])
        for k in writes:
            self.lw[k] = i
            self.rd[k] = ({}, [])
        for k in reads:
            r = self.rd.get(k)
            if r is None:
                r = ({}, [])
                self.rd[k] = r
            if dsem is not None:
                r[1].append(i)
            else:
                r[0][eng] = i
        deps.discard(i)
        self.ops.append(Op(eng, fn, deps, dsem))
        self.last_on_eng[eng] = i
        if dsem is not None:
            self.dma_pending.append(i)
        return i

    def barrier(self):
        deps = set(self.last_on_eng.values()) | set(self.dma_pending)
        self.ops.append(Op("barrier", None, deps, None))
        self.dma_pending = []

    def emit(self):
        nc = self.nc
        ops = self.ops
        for op in ops:
            for d in op.deps:
                if op.eng == "pe" and ops[d].eng == "pe":
                    continue
                ops[d].signal = True
        cnt = {e: 0 for e in self.ENGS}
        dcnt = {}
        for op in ops:
            if op.eng == "barrier":
                continue
            if op.dsem is not None:
                dcnt[op.dsem] = dcnt.get(op.dsem, 0) + 16
                op.count = dcnt[op.dsem]
            elif op.signal:
                cnt[op.eng] += 1
                op.count = cnt[op.eng]
        sems = {}

        def sem(name):
            s = sems.get(name)
            if s is None:
                s = self.es.enter_context(nc.semaphore("s_" + name))
                sems[name] = s
            return s

        known = {e: {} for e in self.ENGS}

        def do_waits(e, deps):
            waits = {}
            for d in deps:
                dop = ops[d]
                if dop.dsem is not None:
                    nm = "d_" + dop.dsem
                else:
                    if dop.eng == "pe" and e == "pe":
                        continue
                    nm = "e_" + dop.eng
                if dop.count > waits.get(nm, 0):
                    waits[nm] = dop.count
            kn = known[e]
            for nm, val in waits.items():
                if kn.get(nm, 0) < val:
                    self.eng[e].wait_ge(sem(nm), val)
                    kn[nm] = val

        for op in ops:
            if op.eng == "barrier":
                for e in self.ENGS:
                    do_waits(e, op.deps)
                continue
            do_waits(op.eng, op.deps)
            ins = op.fn()
            if op.dsem is not None:
                ins.then_inc(sem("d_" + op.dsem), 16)
            elif op.signal:
                ins.then_inc(sem("e_" + op.eng), 1)
        self.stats = dict(n_ops=len(ops), cnt=cnt, n_sems=len(sems))


def build_program():
    nc = bass.Bass("TRN2", target_bir_lowering=False)

    def din(name, shape, dt=F32):
        return nc.dram_tensor(name, list(shape), dt, kind="ExternalInput").ap()

    xT = din("xT", [D, 2048])
    xo = din("xo", [T_OWN, D])
    memT = din("memT", [D, 256])
    WIN = din("WIN", [N_WIN, 128, 16, 128])
    WUQ = din("WUQ", [16, 128, 4, 128])
    WUKV = din("WUKV", [16, 128, 2, 128])
    WMKV = din("WMKV", [8, 128, 16, 128])
    WPA = din("WPA", [16, 128, 8, 128])
    WPB = din("WPB", [16, 128, 8, 128])
    WPM = din("WPM", [16, 128, 4, 128])
    WOUT = din("WOUT", [128, 16, 2048])
    gcols = din("gcols", [128, 40])
    gfin = din("gfin", [1, D])
    cosT = din("cosT", [64, 2048])
    ssinT = din("ssinT", [64, 2048])
    pastb = din("pastb", [128, 64])
    pasti = din("pasti", [128, 64])
    npsel = din("npsel", [128, 64])
    ctxb = din("ctxb", [128, 1])
    E1 = din("E1", [33, 384])
    rbaug = din("rbaug", [33, 8])
    identb = din("identb", [128, 128], BF16)
    SELc = din("SELc", [8, 8 * 128], BF16)
    CAUSc = din("CAUSc", [128, 128], BF16)
    out = nc.dram_tensor("out", [T_OWN, D], F32, kind="ExternalOutput").ap()
    if DUMP:
        dgA = nc.dram_tensor("dgA", [128, 8192], BF16, kind="ExternalOutput").ap()
        dgB = nc.dram_tensor("dgB", [128, 8192], BF16, kind="ExternalOutput").ap()
        dgM = nc.dram_tensor("dgM", [128, 4096], BF16, kind="ExternalOutput").ap()
        dyT = nc.dram_tensor("dyT", [128, 16384], BF16, kind="ExternalOutput").ap()
        dL = nc.dram_tensor("dL", [128, 10240], BF16, kind="ExternalOutput").ap()
        dH = nc.dram_tensor("dH", [128, 16384], BF16, kind="ExternalOutput").ap()
    scr = nc.dram_tensor("scr", [8, 128, 384], F32, kind="Internal")
    scr_ap = scr.ap()

    es = contextlib.ExitStack()
    with es:
        ARENA_KB = 196
        arena = es.enter_context(nc.sbuf_tensor("arena", [128, ARENA_KB * 512], BF16))
        banks = [es.enter_context(nc.psum_tensor(f"bank{i}", [128, 512], F32)) for i in range(8)]
        sch = Sched(nc, es)

        def vb(off_kb, nelem):
            o = int(off_kb * 512)
            return arena[:, o:o + nelem]

        def vf(off_kb, nelem):
            o = int(off_kb * 512)
            return arena[:, o:o + 2 * nelem].bitcast(F32)

        hTo = vb(0, 16 * 1024).rearrange("p (c t) -> p c t", c=16)
        hTc = vb(32, 16 * 1024).rearrange("p (c t) -> p c t", c=16)
        gA = vb(64, 8 * 1024).rearrange("p (c t) -> p c t", c=8)
        gB = vb(80, 8 * 1024).rearrange("p (c t) -> p c t", c=8)
        gM = vb(96, 4 * 1024).rearrange("p (c t) -> p c t", c=4)
        NSLOT = 4
        wsl = [vb(104 + 4 * i, 2048).rearrange("p (c n) -> p c n", c=16) for i in range(NSLOT)]
        ones_b = vb(120, 128)
        ident = vb(120.25, 128)
        caus = vb(120.5, 128)
        SEL = vb(120.75, 1024)
        Whi = vb(122.75, 8 * 256).rearrange("p (h x) -> p h x", h=8)
        Wlo = vb(126.75, 8 * 256).rearrange("p (h x) -> p h x", h=8)
        gc = vf(130.75, 40)
        ctxbias = vf(131, 1)
        zero_c = vf(131.25, 1)
        pastb_t = vf(131.5, 64)
        pasti_t = vf(131.75, 64)
        npsel_t = vf(132, 64)
        cos_own = vf(132.5, 1024)
        ssin_own = vf(136.5, 1024)
        TB = 140.5

        PSB = lambda i: ("ps", i)

        def mm(o, lhsT, rhs, start, stop, reads, writes):
            sch.add("pe", lambda: nc.tensor.matmul(o, lhsT=lhsT, rhs=rhs, start=start, stop=stop), reads, writes)

        def tr(o, in_, idn, reads, writes):
            sch.add("pe", lambda: nc.tensor.transpose(o, in_, idn), reads, writes)

        def act(o, in_, func, reads, writes, scale=1.0, bias=0.0, accum_out=None):
            if accum_out is None:
                sch.add("act", lambda: nc.scalar.activation(out=o, in_=in_, func=func, bias=bias, scale=scale), reads, writes)
            else:
                sch.add("act", lambda: nc.scalar.activation(out=o, in_=in_, func=func, bias=bias, scale=scale, accum_out=accum_out), reads, writes)

        def tt(o, a, b, op, reads, writes, eng="dve"):
            e = nc.vector if eng == "dve" else nc.gpsimd
            sch.add(eng, lambda: e.tensor_tensor(out=o, in0=a, in1=b, op=op), reads, writes)

        def ts(o, a, s1, s2, op0, op1, reads, writes):
            if op1 is None:
                sch.add("dve", lambda: nc.vector.tensor_scalar(out=o, in0=a, scalar1=s1, scalar2=None, op0=op0), reads, writes)
            else:
                sch.add("dve", lambda: nc.vector.tensor_scalar(out=o, in0=a, scalar1=s1, scalar2=s2, op0=op0, op1=op1), reads, writes)

        def stt(o, a, s, b, op0, op1, reads, writes):
            sch.add("dve", lambda: nc.vector.scalar_tensor_tensor(out=o, in0=a, scalar=s, in1=b, op0=op0, op1=op1), reads, writes)

        def cp(o, a, reads, writes):
            sch.add("dve", lambda: nc.vector.tensor_copy(out=o, in_=a), reads, writes)

        def recip(o, a, reads, writes):
            sch.add("dve", lambda: nc.vector.reciprocal(out=o, in_=a), reads, writes)

        def memset(o, val, writes):
            sch.add("dve", lambda: nc.vector.memset(o, val), (), writes)

        def dma(q, o, in_, dsem, reads, writes):
            e = {"sp": nc.sync, "pool": nc.gpsimd}[q]
            sch.add(q, lambda: e.dma_start(out=o, in_=in_), reads, writes, dsem=dsem)

        memset(ones_b, 1.0, [("ones",)])
        memset(zero_c, 0.0, [("zero",)])
        dma("sp", ident, identb, "c_ident", (), [("ident",)])
        dma("sp", caus, CAUSc, "c_caus", (), [("caus",)])
        memset(SEL, 0.0, [("SEL",)])
        dma("sp", SEL[0:8, :], SELc, "c_sel", (), [("SEL",)])
        dma("sp", gc, gcols, "c_gc", (), [("gc",)])
        dma("sp", ctxbias, ctxb, "c_ctxb", (), [("ctxb",)])
        dma("sp", pastb_t, pastb, "c_pb", (), [("pastb",)])
        dma("sp", pasti_t, pasti, "c_pi", (), [("pasti",)])
        dma("sp", npsel_t, npsel, "c_np", (), [("npsel",)])
        dma("sp", cos_own[0:64, :], cosT[:, 1024:2048], "c_cos", (), [("cos_own",)])
        dma("sp", ssin_own[0:64, :], ssinT[:, 1024:2048], "c_sin", (), [("ssin_own",)])

        ts(cos_own[0:64, :], cos_own[0:64, :], SC_MLA, None, ALU.mult, None, [("cos_own",)], [("cos_own",)])
        ts(ssin_own[0:64, :], ssin_own[0:64, :], SC_MLA, None, ALU.mult, None, [("ssin_own",)], [("ssin_own",)])

        wstate = {"n": 0}

        def wload(src_ap, kc):
            s = wstate["n"] % NSLOT
            wstate["n"] += 1
            key = ("w", s)
            dma("pool", wsl[s][:, 0:kc, :], src_ap, f"w{s}", (), [key])
            return wsl[s], key

        class WStream:
            def __init__(self, specs, look=3):
                self.specs = specs
                self.look = look
                self.loaded = []

            def get(self, i):
                while len(self.loaded) < min(len(self.specs), i + self.look):
                    src, kc = self.specs[len(self.loaded)]
                    self.loaded.append(wload(src, kc))
                return self.loaded[i]

        pbank = {"n": 0, "list": [4, 5, 6, 7]}

        def next_pbank():
            l = pbank["list"]
            b = l[pbank["n"] % len(l)]
            pbank["n"] += 1
            return b

        def proj(wv, wkey, kc_n, rhs_fn, evac, m_lo=0, m_hi=128, ncols=512):
            b = next_pbank()
            M = m_hi - m_lo
            o = banks[b][0:M, 0:ncols]
            for kc in range(kc_n):
                r_ap, r_key = rhs_fn(kc)
                mm(o, wv[:, kc, m_lo:m_hi], r_ap, kc == 0, kc == kc_n - 1, [wkey, r_key], [PSB(b)])
            evac(o, PSB(b))

        def hT(kc, tg):
            if tg < 2:
                return hTc[:, kc, tg * 512:(tg + 1) * 512], ("hTc", kc, tg)
            return hTo[:, kc, (tg - 2) * 512:(tg - 1) * 512], ("hTo", kc, tg - 2)

        def rms_stage(src, ntg, ncols, gcol0, dst_fn, TBo, alt_off=None):
            xs_sets = [[vf(TBo + 2 * c, 512) for c in range(16)]]
            xs_offs = [TBo]
            if alt_off is not None:
                xs_sets.append([vf(alt_off + 2 * c, 512) for c in range(16)])
                xs_offs.append(alt_off)
            rstd = vf(TBo + 32, 512)
            sqs = [vb(TBo + 34 + i, 512) for i in range(3)]
            tmp = vf(TBo + 37, 512)
            for tg in range(ntg):
                b = next_pbank()
                par = tg % len(xs_sets)
                xs = xs_sets[par]
                if False and ncols == 512:
                    base = xs_offs[par]
                    for cg in range(4):
                        dst4 = vf(base + 8 * cg, 4 * 512).rearrange("p (c t) -> p c t", c=4)
                        src4 = src[cg * 512:(cg + 1) * 512, tg * ncols:(tg + 1) * ncols].rearrange("(c p) t -> p c t", p=128)
                        dma("sp", dst4, src4, f"xs{par}_{cg}", (), [("xs", par, 4 * cg + i) for i in range(4)])
                for c in range(16):
                    if True:
                        dma("sp", xs[c][:, 0:ncols], src[c * 128:(c + 1) * 128, tg * ncols:(tg + 1) * ncols],
                            f"xs{par}_{c}", (), [("xs", par, c)])
                    sq = sqs[c % 3]
                    if DBG < 2:
                        continue
                    act(sq[:, 0:ncols], xs[c][:, 0:ncols], AF.Square, [("xs", par, c)], [("sq", c % 3)])
                    if DBG < 3:
                        continue
                    mm(banks[b][:, 0:ncols], ones_b, sq[:, 0:ncols], c == 0, c == 15,
                       [("ones",), ("sq", c % 3)], [PSB(b)])
                if DBG < 4:
                    continue
                ts(tmp[:, 0:ncols], banks[b][:, 0:ncols], 1.0 / D, EPS, ALU.mult, ALU.add, [PSB(b)], [("rtmp",)])
                act(tmp[:, 0:ncols], tmp[:, 0:ncols], AF.Sqrt, [("rtmp",)], [("rtmp",)])
                recip(rstd[:, 0:ncols], tmp[:, 0:ncols], [("rtmp",)], [("rstd",)])
                if DBG < 5:
                    continue
                for c in range(16):
                    d_ap, d_key = dst_fn(c, tg)
                    stt(d_ap, xs[c][:, 0:ncols], gc[:, gcol0 + c:gcol0 + c + 1], rstd[:, 0:ncols],
                        ALU.mult, ALU.mult, [("xs", par, c), ("gc",), ("rstd",)], [d_key])

        if STOP >= 1:
            rms_stage(xT, 4, 512, 0, hT, TB, alt_off=64)

        TW = TB + 40
        e1_t = vf(TW, 384)
        rb_t = vf(TW + 1.5, 8)
        rbrep = vf(TW + 1.75, 128)
        ones_f = vf(TW + 2.25, 128)
        vrep = vf(TW + 2.75, 384)
        wf = vf(TW + 4.5, 256)
        wd = vf(TW + 5.5, 256)
        dma("sp", e1_t[0:33, :], E1, "c_e1", (), [("e1",)])
        dma("sp", rb_t[0:33, :], rbaug, "c_rb", (), [("rb",)])
        memset(ones_f[0:33, :], 1.0, [("ones_f",)])
        for h in range(8 if STOP >= 2 else 0):
            ts(rbrep[0:33, :], ones_f[0:33, :], rb_t[0:33, h:h + 1], None, ALU.mult, None,
               [("ones_f",), ("rb",)], [("rbrep",)])
            b = next_pbank()
            mm(banks[b][:, 0:384], rbrep[0:33, :], e1_t[0:33, :], True, True, [("rbrep",), ("e1",)], [PSB(b)])
            cp(vrep, banks[b][:, 0:384], [PSB(b)], [("vrep",)])
            dma("sp", scr_ap[h], vrep, "scrw", [("vrep",)], [("scr", h)])
            skew = bass.AP(scr, h * 128 * 384 + 127, [[383, 128], [1, 256]])
            dma("sp", wf, skew, "scrr", [("scr", h)], [("wf",)])
            cp(Whi[:, h, :], wf, [("wf",)], [("Whi", h)])
            tt(wd, wf, Whi[:, h, :], ALU.subtract, [("wf",), ("Whi", h)], [("wd",)])
            cp(Wlo[:, h, :], wd, [("wd",)], [("Wlo", h)])

        sch.barrier()

        PT = [vb(TB + i, 512) for i in range(4)]
        rden = [vf(TB + 4, 512), vf(TB + 190.5 - 140.5, 512)]
        otmp = [vf(TB + 6, 512), vf(TB + 192.5 - 140.5, 512)]
        pacc = [vf(183, 512), vf(185, 512)]
        ones_f32 = vf(194.5, 128)
        memset(ones_f32, 1.0, [("ones_f32",)])
        att = {"n": 0, "pair": 0, "S": [0, 1], "LA": 1, "pairs": [(2, 3), (4, 5)]}

        def attention(n_ktiles_fn, logit_mms, exp_bias_fn, v_fn, dst_fn, siluz_fn, col0_fn):
            LA = att["LA"]
            Sb = att["S"]
            for G in range(2):
                pr = att["pair"] % 2
                ob, db = att["pairs"][pr]
                att["pair"] += 1
                nk = n_ktiles_fn(G)
                info = {}
                for jj in range(nk + LA):
                    if jj < nk:
                        j = jj
                        c0 = col0_fn(G, j)
                        N = 512 - c0
                        sb = Sb[att["n"] % len(Sb)]
                        slot = att["n"] % 4
                        att["n"] += 1
                        logit_mms(G, j, c0, banks[sb][:, 0:N], PSB(sb))
                        bias, bkey = exp_bias_fn(j)
                        act(PT[slot][:, 0:N], banks[sb][:, 0:N], AF.Exp, [PSB(sb)] + ([bkey] if bkey else []),
                            [("PT", slot)], bias=bias)
                        info[j] = (c0, N, slot)
                    if jj >= LA:
                        j = jj - LA
                        c0, N, slot = info[j]
                        v_ap, v_key = v_fn(j)
                        mm(banks[ob][:, c0:512], v_ap, PT[slot][:, 0:N], j == 0, j == nk - 1, [v_key, ("PT", slot)], [PSB(ob)])
                        if j == 0:
                            sch.add("pool", lambda slot=slot, pr=pr: nc.gpsimd.tensor_copy(out=pacc[pr], in_=PT[slot][:, 0:512]),
                                    [("PT", slot)], [("pacc", pr)])
                        else:
                            sch.add("pool", lambda slot=slot, pr=pr, c0=c0, N=N: nc.gpsimd.tensor_tensor(
                                out=pacc[pr][:, c0:512], in0=pacc[pr][:, c0:512], in1=PT[slot][:, 0:N], op=ALU.add),
                                [("PT", slot), ("pacc", pr)], [("pacc", pr)])
                    yield
                mm(banks[db][:, :], ones_f32, pacc[pr], True, True, [("ones_f32",), ("pacc", pr)], [PSB(db)])
                recip(rden[pr], banks[db][:, :], [PSB(db)], [("rden", pr)])
                tt(otmp[pr], banks[ob][:, :], rden[pr], ALU.mult, [PSB(ob), ("rden", pr)], [("otmp", pr)])
                d_ap, d_key = dst_fn(G)
                z_ap, z_key = siluz_fn(G)
                tt(d_ap, otmp[pr], z_ap, ALU.mult, [("otmp", pr), z_key], [d_key])

        def interleave(ga, gp, na=2, npj=1):
            a_done = ga is None
            p_done = gp is None
            while not (a_done and p_done):
                for _ in range(na):
                    if not a_done:
                        try:
                            next(ga)
                        except StopIteration:
                            a_done = True
                for _ in range(npj):
                    if not p_done:
                        try:
                            next(gp)
                        except StopIteration:
                            p_done = True

        def transpose_v(vT, vT_key, ntiles, dst_fn):
            for j4 in range(0, ntiles, 4):
                n = min(4, ntiles - j4)
                b = next_pbank()
                pb = banks[b][:, :].bitcast(BF16)
                for jj in range(n):
                    tr(pb[:, jj * 128:(jj + 1) * 128], vT[:, (j4 + jj) * 128:(j4 + jj + 1) * 128], ident,
                       [vT_key, ("ident",)], [PSB(b)])
                d_ap, d_key = dst_fn(j4, n)
                cp(d_ap, pb[:, 0:n * 128], [PSB(b)], [d_key])

        MB = TB + 8
        kT = [vb(MB + 4 * i, 2048) for i in range(2)]
        vtok = [vb(MB + 8 + 4 * i, 2048) for i in range(2)]
        qT = [vb(MB + 16 + 2 * i, 1024) for i in range(2)]
        szT = [vb(MB + 20 + 2 * i, 1024) for i in range(2)]
        vT_tmp = vb(MB + 24, 2048)
        selbT = [vb(MB + 28 + 2 * i, 1024) for i in range(2)]
        ksum = vf(MB + 32, 8)
        kmean_b = vb(MB + 32.25, 8)
        gm = vf(MB + 32.5, 64)
        mx8 = vf(MB + 33, 8)
        selb1 = vf(MB + 33.25, 64)
        selb2 = vf(MB + 33.75, 64)
        selb3 = vb(MB + 34.25, 64)

        pbank["list"] = [6, 7]
        for i in range(2):
            memset(selbT[i], 0.0, [("selbT", i)])
        specs = []
        for h in range(8):
            for r in range(4):
                specs.append((WIN[CH_MOBA + 4 * h + r], 16))
        ws = WStream(specs)
        def b1_proj(h):
            p = h % 2
            kT_h, vt_h, qT_h, sz_h, sb_h = kT[p], vtok[p], qT[p], szT[p], selbT[p]
            wv, wk = ws.get(4 * h + 0)
            for tg in range(4):
                def ev_k(o, ok, tg=tg):
                    for hh in range(2):
                        act(kT_h[:, tg * 512 + hh * 256: tg * 512 + (hh + 1) * 256], o[:, hh * 256:(hh + 1) * 256], AF.Copy,
                            [ok], [("kT", p, tg), ("ksum", tg)], accum_out=ksum[:, 2 * tg + hh:2 * tg + hh + 1])
                proj(wv, wk, 16, lambda kc, tg=tg: hT(kc, tg), ev_k)
                yield
            ts(kmean_b, ksum, 1.0 / 256, None, ALU.mult, None, [("ksum", t) for t in range(4)], [("kmean",)])
            wv, wk = ws.get(4 * h + 1)
            for tg in range(4):
                def ev_v(o, ok, tg=tg):
                    cp(vT_tmp[:, tg * 512:(tg + 1) * 512], o, [ok], [("vT", tg)])
                proj(wv, wk, 16, lambda kc, tg=tg: hT(kc, tg), ev_v)
                yield
            for tg in range(4):
                transpose_v(vT_tmp[:, tg * 512:(tg + 1) * 512], ("vT", tg), 4,
                            lambda j4, n, tg=tg: (vt_h[:, (tg * 4) * 128:(tg * 4 + 4) * 128], ("vtok", p, tg)))
                yield
            wv, wk = ws.get(4 * h + 2)
            for tg in range(2):
                def ev_q(o, ok, tg=tg):
                    act(qT_h[:, tg * 512:(tg + 1) * 512], o, AF.Copy, [ok], [("qT", p, tg)], scale=SC_MOBA)
                proj(wv, wk, 16, lambda kc, tg=tg: hT(kc, tg + 2), ev_q)
                yield
            wv, wk = ws.get(4 * h + 3)
            for tg in range(2):
                def ev_z(o, ok, tg=tg):
                    act(sz_h[:, tg * 512:(tg + 1) * 512], o, AF.Silu, [ok], [("szT", p, tg)])
                proj(wv, wk, 16, lambda kc, tg=tg: hT(kc, tg + 2), ev_z)
                yield
            b = next_pbank()
            for i in range(8):
                mm(banks[b][:, i * 8:(i + 1) * 8], qT_h[:, i * 128:(i + 1) * 128], kmean_b, True, True,
                   [("qT", p, i // 4), ("kmean",)], [PSB(b)])
            tt(gm, banks[b][:, 0:64], pastb_t, ALU.add, [PSB(b), ("pastb",)], [("gm",)])
            for i in range(8):
                sl = slice(i * 8, (i + 1) * 8)
                sch.add("dve", lambda sl=sl: nc.vector.max(out=mx8, in_=gm[:, sl]), [("gm",)], [("mx8",)])
                ts(selb1[:, sl], gm[:, sl], mx8[:, 2:3], NEG, ALU.is_lt, ALU.mult, [("gm",), ("mx8",)], [("selb1", i)])
            tt(selb2, selb1, pasti_t, ALU.mult, [("selb1", i) for i in range(8)] + [("pasti",)], [("selb2",)])
            tt(selb3, selb2, npsel_t, ALU.add, [("selb2",), ("npsel",)], [("selb3",)])
            yield
            b = next_pbank()
            pb = banks[b][:, :].bitcast(BF16)
            for i in range(8):
                tr(pb[0:8, i * 128:(i + 1) * 128], selb3[:, i * 8:(i + 1) * 8], ident, [("selb3",), ("ident",)], [PSB(b)])
            cp(sb_h[0:8, :], pb[0:8, 0:1024], [PSB(b)], [("selbT", p)])
            yield

        def b1_attn(h):
            p = h % 2
            kT_h, vt_h, qT_h, sz_h, sb_h = kT[p], vtok[p], qT[p], szT[p], selbT[p]

            def lm(G, j, c0, o, ok):
                N = 512 - c0
                q0 = 512 * G + c0
                n = j // 2
                ip = j - 8
                extra = []
                for (qt, xo_) in ((ip, 0), (ip + 1, 128)):
                    if qt < 0 or qt > 7:
                        continue
                    cc = qt * 128 - 512 * G
                    if cc < c0 or cc >= 512:
                        continue
                    extra.append((cc - c0, xo_))
                mm(o, kT_h[:, j * 128:(j + 1) * 128], qT_h[:, q0:q0 + N], True, False,
                   [("kT", p, j // 4), ("qT", p, G)], [ok])
                for (cc, xo_) in extra:
                    mm(o[:, cc:cc + 128], ident, Whi[:, h, xo_:xo_ + 128], False, False, [("ident",), ("Whi", h)], [ok])
                    mm(o[:, cc:cc + 128], ident, Wlo[:, h, xo_:xo_ + 128], False, False, [("ident",), ("Wlo", h)], [ok])
                mm(o, SEL[:, n * 128:(n + 1) * 128], sb_h[:, q0:q0 + N], False, True,
                   [("SEL",), ("selbT", p)], [ok])

            yield from attention(lambda G: 8 + 4 * G + 4, lm, lambda j: (0.0, None),
                                 lambda j: (vt_h[:, j * 128:(j + 1) * 128], ("vtok", p, j // 4)),
                                 lambda G: (gA[:, h, G * 512:(G + 1) * 512], ("gA", h, G)),
                                 lambda G: (sz_h[:, G * 512:(G + 1) * 512], ("szT", p, G)),
                                 lambda G, j: max(0, j - 8 - 4 * G) * 128)

        nh1 = NH1 if STOP >= 3 else 0
        if nh1 > 0:
            for _ in b1_proj(0):
                pass
        for h in range(nh1):
            interleave(b1_attn(h), b1_proj(h + 1) if h + 1 < nh1 else None, 3, 2)

        sch.barrier()

        pbank["list"] = [4, 5, 6, 7]
        LB = TB + 8
        ckvn = vb(LB, 2 * 2048).rearrange("p (c t) -> p c t", c=2)
        cqn = vb(LB + 8, 4 * 1024).rearrange("p (c t) -> p c t", c=4)
        kropeT = vb(LB + 16, 2048)
        sq2 = [vb(LB + 20 + i, 512) for i in range(2)]
        rtmp2 = vf(LB + 22, 512)
        rstd2 = vf(LB + 24, 512)
        cs_t = vf(LB + 26, 512)
        sn_t = vf(LB + 28, 512)
        t1 = vf(LB + 30, 512)
        t2 = vf(LB + 32, 512)

        memset(kropeT, 0.0, [("kropeT", t) for t in range(4)])
        specs = [(WIN[CH_CKV], 16), (WIN[CH_CKV + 1], 16), (WIN[CH_KR], 16)] + [(WIN[CH_CQ + c], 16) for c in range(4)]
        ws = WStream(specs, look=3)

        def latent(nch, ws_i0, tgs, tg_off, gcol0, dst, dst_name, dim):
            for tg in tgs:
                sb_ = 1
                for c in range(nch):
                    wv, wk = ws.get(ws_i0 + c)

                    def ev(o, ok, c=c, tg=tg):
                        ts(dst[:, c, tg * 512:(tg + 1) * 512], o, gc[:, gcol0 + c:gcol0 + c + 1], None, ALU.mult, None,
                           [ok, ("gc",)], [(dst_name, c, tg)])
                        if LDBG < 1:
                            return
                        act(sq2[c % 2], o, AF.Square, [ok], [("sq2", c % 2)])
                        if LDBG < 2:
                            return
                        mm(banks[sb_][:, :], ones_b, sq2[c % 2], c == 0, c == nch - 1, [("ones",), ("sq2", c % 2)], [PSB(sb_)])
                    proj(wv, wk, 16, lambda kc, tg=tg: hT(kc, tg + tg_off), ev)
                if LDBG < 3:
                    continue
                ts(rtmp2, banks[sb_][:, :], 1.0 / dim, EPS, ALU.mult, ALU.add, [PSB(sb_)], [("rtmp2",)])
                act(rtmp2, rtmp2, AF.Sqrt, [("rtmp2",)], [("rtmp2",)])
                recip(rstd2, rtmp2, [("rtmp2",)], [("rstd2",)])
                if LDBG < 4:
                    continue
                for c in range(nch):
                    tt(dst[:, c, tg * 512:(tg + 1) * 512], dst[:, c, tg * 512:(tg + 1) * 512], rstd2, ALU.mult,
                       [(dst_name, c, tg), ("rstd2",)], [(dst_name, c, tg)])

        if STOP < 4:
            sch.barrier()
            sch.emit()
            return nc, sch
        latent(2, 0, range(4), 0, 36, ckvn, "ckvn", 256)

        def rope_pair(wv, wk, kc_n, rhs_fn, cos_ap, cos_key, sin_ap, sin_key, dst_ap, dst_key, scale):
            hold = {}

            def evA(o, ok):
                tt(t1[0:64, :], o, cos_ap, ALU.mult, [ok, cos_key], [("t1",)])

            def evB(o, ok):
                tt(t2[0:64, :], o, sin_ap, ALU.mult, [ok, sin_key], [("t2",)])
            proj(wv, wk, kc_n, rhs_fn, evA, 0, 64)
            proj(wv, wk, kc_n, rhs_fn, evB, 64, 128)
            stt(dst_ap, t1[0:64, :], scale, t2[0:64, :], ALU.mult, ALU.add, [("t1",), ("t2",)], [dst_key])

        wv, wk = ws.get(2)
        for tg in range(4 if DBG >= 21 else 0):
            dma("sp", cs_t[0:64, :], cosT[:, tg * 512:(tg + 1) * 512], "cs_t", (), [("cs_t",)])
            dma("sp", sn_t[0:64, :], ssinT[:, tg * 512:(tg + 1) * 512], "sn_t", (), [("sn_t",)])
            rope_pair(wv, wk, 16, lambda kc, tg=tg: hT(kc, tg), cs_t[0:64, :], ("cs_t",), sn_t[0:64, :], ("sn_t",),
                      kropeT[0:64, tg * 512:(tg + 1) * 512], ("kropeT", tg), 1.0)

        if DBG >= 22:
            latent(4, 3, range(2), 2, 32, cqn, "cqn", 512)

        sch.barrier()

        pbank["list"] = [6, 7]
        att.update(S=[0, 1], LA=1, pairs=[(2, 3), (4, 5)])
        HB = 32
        knT = [vb(HB + 4 * i, 2048) for i in range(2)]
        vtb = [vb(HB + 8 + 4 * i, 2048) for i in range(2)]
        qnT = [vb(HB + 16 + 2 * i, 1024) for i in range(2)]
        qrT = [vb(HB + 20 + 2 * i, 1024) for i in range(2)]
        szB = [vb(HB + 24 + 2 * i, 1024) for i in range(2)]
        vT2 = vb(HB + 28, 2048)

        for i in range(2):
            memset(qrT[i], 0.0, [("qrT", i, 0), ("qrT", i, 1)])
        specs = []
        for h in range(8):
            specs += [(WUKV[2 * h], 2), (WUKV[2 * h + 1], 2), (WUQ[2 * h], 4), (WUQ[2 * h + 1], 4), (WIN[CH_ZB + h], 16)]
        ws = WStream(specs, look=4)
        if STOP < 5:
            sch.barrier()
            sch.emit()
            return nc, sch
        def b3_proj(h):
            p = h % 2
            kn_h, vt_h, qn_h, qr_h, sz_h = knT[p], vtb[p], qnT[p], qrT[p], szB[p]
            wv, wk = ws.get(5 * h + 0)
            for tg in range(4):
                def ev_k(o, ok, tg=tg):
                    act(kn_h[:, tg * 512:(tg + 1) * 512], o, AF.Copy, [ok], [("knT", p, tg)])
                proj(wv, wk, 2, lambda kc, tg=tg: (ckvn[:, kc, tg * 512:(tg + 1) * 512], ("ckvn", kc, tg)), ev_k)
                yield
            wv, wk = ws.get(5 * h + 1)
            for tg in range(4):
                def ev_v(o, ok, tg=tg):
                    cp(vT2[:, tg * 512:(tg + 1) * 512], o, [ok], [("vT2", tg)])
                proj(wv, wk, 2, lambda kc, tg=tg: (ckvn[:, kc, tg * 512:(tg + 1) * 512], ("ckvn", kc, tg)), ev_v)
                yield
            for tg in range(4):
                transpose_v(vT2[:, tg * 512:(tg + 1) * 512], ("vT2", tg), 4,
                            lambda j4, n, tg=tg: (vt_h[:, (tg * 4) * 128:(tg * 4 + 4) * 128], ("vtb", p, tg)))
                yield
            wv, wk = ws.get(5 * h + 2)
            for tg in range(2):
                def ev_q(o, ok, tg=tg):
                    act(qn_h[:, tg * 512:(tg + 1) * 512], o, AF.Copy, [ok], [("qnT", p, tg)], scale=SC_MLA)
                proj(wv, wk, 4, lambda kc, tg=tg: (cqn[:, kc, tg * 512:(tg + 1) * 512], ("cqn", kc, tg)), ev_q)
                yield
            wv, wk = ws.get(5 * h + 3)
            for tg in range(2):
                rope_pair(wv, wk, 4, lambda kc, tg=tg: (cqn[:, kc, tg * 512:(tg + 1) * 512], ("cqn", kc, tg)),
                          cos_own[0:64, tg * 512:(tg + 1) * 512], ("cos_own",),
                          ssin_own[0:64, tg * 512:(tg + 1) * 512], ("ssin_own",),
                          qr_h[0:64, tg * 512:(tg + 1) * 512], ("qrT", p, tg), 1.0)
                yield
            wv, wk = ws.get(5 * h + 4)
            for tg in range(2):
                def ev_z(o, ok, tg=tg):
                    act(sz_h[:, tg * 512:(tg + 1) * 512], o, AF.Silu, [ok], [("szB", p, tg)])
                proj(wv, wk, 16, lambda kc, tg=tg: hT(kc, tg + 2), ev_z)
                yield

        def b3_attn(h):
            p = h % 2
            kn_h, vt_h, qn_h, qr_h, sz_h = knT[p], vtb[p], qnT[p], qrT[p], szB[p]

            def lm(G, j, c0, o, ok):
                N = 512 - c0
                q0 = 512 * G + c0
                diag = (j >= 8 and 0 <= (j - 8) * 128 - 512 * G < 512)
                mm(o, kn_h[:, j * 128:(j + 1) * 128], qn_h[:, q0:q0 + N], True, False,
                   [("knT", p, j // 4), ("qnT", p, G)], [ok])
                if diag:
                    cc = (j - 8) * 128 - 512 * G - c0
                    mm(o[:, cc:cc + 128], ident, caus, False, False, [("ident",), ("caus",)], [ok])
                mm(o, kropeT[:, j * 128:(j + 1) * 128], qr_h[:, q0:q0 + N], False, True,
                   [("kropeT", j // 4), ("qrT", p, G)], [ok])

            yield from attention(lambda G: 8 + 4 * G + 4, lm,
                                 lambda j: ((ctxbias, ("ctxb",)) if j < 8 else (0.0, None)),
                                 lambda j: (vt_h[:, j * 128:(j + 1) * 128], ("vtb", p, j // 4)),
                                 lambda G: (gB[:, h, G * 512:(G + 1) * 512], ("gB", h, G)),
                                 lambda G: (sz_h[:, G * 512:(G + 1) * 512], ("szB", p, G)),
                                 lambda G, j: max(0, j - 8 - 4 * G) * 128)

        for _ in b3_proj(0):
            pass
        for h in range(8):
            interleave(b3_attn(h), b3_proj(h + 1) if h + 1 < 8 else None, 3, 2)

        sch.barrier()
        if DUMP:
            dma("sp", dL, vb(LB, 10240), "dump4", (), [("dump", 4)])
            dma("sp", dH, vb(HB, 16384), "dump5", (), [("dump", 5)])
            sch.barrier()

        mT = vb(HB, 16 * 256).rearrange("p (c t) -> p c t", c=16)
        kmT = vb(HB + 8, 4 * 256).rearrange("p (c t) -> p c t", c=4)
        vmtok = vb(HB + 10, 2 * 512)
        qmT = vb(HB + 12, 1024)
        szM = vb(HB + 14, 1024)
        vmT = vb(HB + 16, 256)

        if STOP < 6:
            sch.barrier()
            sch.emit()
            return nc, sch

        def mT_dst(c, tg):
            return mT[:, c, :], ("mT", c)
        rms_stage(memT, 1, 256, 16, mT_dst, TB + 8)

        specs = []
        for hm in range(4):
            specs += [(WMKV[hm], 16), (WMKV[4 + hm], 16), (WIN[CH_QM + hm], 16), (WIN[CH_ZM + hm], 16)]
        ws = WStream(specs)
        for hm in range(4):
            wv, wk = ws.get(4 * hm + 0)

            def ev_k(o, ok, hm=hm):
                act(kmT[:, hm, :], o, AF.Copy, [ok], [("kmT", hm)])
            proj(wv, wk, 16, lambda kc: (mT[:, kc, :], ("mT", kc)), ev_k, ncols=256)
            wv, wk = ws.get(4 * hm + 1)

            def ev_v(o, ok):
                cp(vmT, o, [ok], [("vmT",)])
            proj(wv, wk, 16, lambda kc: (mT[:, kc, :], ("mT", kc)), ev_v, ncols=256)
            b = next_pbank()
            pb = banks[b][:, :].bitcast(BF16)
            for jj in range(2):
                tr(pb[:, jj * 128:(jj + 1) * 128], vmT[:, jj * 128:(jj + 1) * 128], ident, [("vmT",), ("ident",)], [PSB(b)])
            for jj in range(2):
                cp(vmtok[:, jj * 512 + hm * 128: jj * 512 + (hm + 1) * 128], pb[:, jj * 128:(jj + 1) * 128],
                   [PSB(b)], [("vmtok", jj, hm)])
            wv, wk = ws.get(4 * hm + 2)
            for tg in range(2):
                def ev_q(o, ok, tg=tg):
                    act(qmT[:, tg * 512:(tg + 1) * 512], o, AF.Copy, [ok], [("qmT", tg)], scale=SC_MEM)
                proj(wv, wk, 16, lambda kc, tg=tg: hT(kc, tg + 2), ev_q)
            wv, wk = ws.get(4 * hm + 3)
            for tg in range(2):
                def ev_z(o, ok, tg=tg):
                    act(szM[:, tg * 512:(tg + 1) * 512], o, AF.Silu, [ok], [("szM", tg)])
                proj(wv, wk, 16, lambda kc, tg=tg: hT(kc, tg + 2), ev_z)

            def lm(G, j, c0, o, ok, hm=hm):
                mm(o, kmT[:, hm, j * 128:(j + 1) * 128], qmT[:, 512 * G:512 * G + 512], True, True,
                   [("kmT", hm), ("qmT", G)], [ok])

            for _ in attention(lambda G: 2, lm, lambda j: (0.0, None),
                      lambda j, hm=hm: (vmtok[:, j * 512 + hm * 128: j * 512 + (hm + 1) * 128], ("vmtok", j, hm)),
                      lambda G, hm=hm: (gM[:, hm, G * 512:(G + 1) * 512], ("gM", hm, G)),
                      lambda G: (szM[:, G * 512:(G + 1) * 512], ("szM", G)),
                      lambda G, j: 0):
                pass

        sch.barrier()

        if STOP < 7:
            sch.barrier()
            sch.emit()
            return nc, sch
        pbank["list"] = [0, 1, 2, 3, 4, 5, 6, 7]
        yT = vb(TB + 8, 16 * 1024).rearrange("p (c t) -> p c t", c=16)
        sig = [vf(32 + 2 * i, 512) for i in range(3)]
        yacc0 = [vf(38 + 2 * i, 512) for i in range(2)]
        yacc1 = [vf(42 + 2 * i, 512) for i in range(2)]
        ytmp = [vf(TB + 2 * i, 512) for i in range(2)]
        wo_slots = [vb(46, 16 * 512).rearrange("p (c n) -> p c n", c=16), vb(62, 16 * 512).rearrange("p (c n) -> p c n", c=16)]
        gsrc = [(gA, "gA", 8, WPA), (gB, "gB", 8, WPB), (gM, "gM", 4, WPM)]
        specs = []
        for c in range(16):
            for br in range(3):
                specs += [(WIN[CH_GL + 16 * br + c], 16), (gsrc[br][3][c], gsrc[br][2])]
        ws = WStream(specs)
        for c in range(16):
            if c == 12:
                wo_load(dma, wo_slots[0], WOUT, 0)
            for br in range(3):
                gten, gname, gk, _ = gsrc[br]
                wvg, wkg = ws.get((c * 3 + br) * 2)
                wvp, wkp = ws.get((c * 3 + br) * 2 + 1)
                for G in range(2):
                    def ev_gl(o, ok, br=br):
                        act(sig[br], o, AF.Sigmoid, [ok], [("sig", br)])
                    proj(wvg, wkg, 16, lambda kc, G=G: hT(kc, G + 2), ev_gl)

                    def ev_p(o, ok, br=br, G=G, c=c):
                        if br == 0:
                            tt(yacc0[G], o, sig[0], ALU.mult, [ok, ("sig", 0)], [("yacc0", G)])
                        elif br == 1:
                            tt(ytmp[0], o, sig[1], ALU.mult, [ok, ("sig", 1)], [("ytmp", 0)])
                            tt(yacc1[G], yacc0[G], ytmp[0], ALU.add, [("yacc0", G), ("ytmp", 0)], [("yacc1", G)])
                        else:
                            tt(ytmp[1], o, sig[2], ALU.mult, [ok, ("sig", 2)], [("ytmp", 1)])
                            tt(yT[:, c, G * 512:(G + 1) * 512], yacc1[G], ytmp[1], ALU.add,
                               [("yacc1", G), ("ytmp", 1)], [("yT", c, G)])
                    proj(wvp, wkp, gk, lambda kc, G=G, gten=gten, gname=gname: (gten[:, kc, G * 512:(G + 1) * 512], (gname, kc, G)), ev_p)

        sch.barrier()
        if DUMP:
            dma("sp", dgA, vb(64, 8192), "dump0", (), [("dump", 0)])
            dma("sp", dgB, vb(80, 8192), "dump1", (), [("dump", 1)])
            dma("sp", dgM, vb(96, 4096), "dump2", (), [("dump", 2)])
            dma("sp", dyT, vb(TB + 8, 16384), "dump3", (), [("dump", 3)])
            sch.barrier()
        if STOP < 8:
            sch.emit()
            return nc, sch
        return_stage_e(nc, sch, vb, vf, banks, yT, WOUT, gfin, xo, out, act, tt, stt, recip, mm, dma, PSB, wo_slots)
        sch.barrier()
        sch.emit()
    return nc, sch


def return_stage_e(nc, sch, vb, vf, banks, yT, WOUT, gfin, xo, out, act, tt, stt, recip, mm, dma, PSB, wo_slots):
    xr = vf(80, 8 * 2048).rearrange("p (t n) -> p t n", t=8)
    gfb = vf(0, 2048)
    ot = [vf(8 + 8 * i, 2048) for i in range(2)]
    sqj = vb(24, 2048)
    ssum = vf(28, 1)
    rs1 = vf(28.25, 1)
    rs2 = vf(28.5, 1)
    dma("sp", gfb, gfin.partition_broadcast(128), "c_gf", (), [("gfb",)])
    for t in range(8):
        dma("sp", xr[:, t, :], xo[t * 128:(t + 1) * 128, :], f"xr{t}", (), [("xr", t, g) for g in range(4)])
    nb = 0
    for g in range(4):
        wv = wo_slots[g % 2]
        if g >= 1:
            wo_load(dma, wv, WOUT, g)
        if g + 1 < 4 and g + 1 >= 2:
            pass
        for t in range(8):
            b = nb % 4
            nb += 1
            for kc in range(16):
                mm(banks[b][:, :], yT[:, kc, t * 128:(t + 1) * 128], wv[:, kc, :],
                   kc == 0, kc == 15, [("yT", kc, t // 4), ("wo", g % 2)], [PSB(b)])
            tt(xr[:, t, g * 512:(g + 1) * 512], banks[b][:, :], xr[:, t, g * 512:(g + 1) * 512], ALU.add,
               [PSB(b), ("xr", t, g)], [("xr", t, g)])
            if g == 3:
                p = t % 2
                rk = [("xr", t, gg) for gg in range(4)]
                act(sqj, xr[:, t, :], AF.Square, rk, [("sqj",), ("ssum",)], accum_out=ssum)
                sch.add("dve", lambda: nc.vector.tensor_scalar(out=rs1, in0=ssum, scalar1=1.0 / D, scalar2=EPS, op0=ALU.mult, op1=ALU.add),
                        [("ssum",)], [("rs1",)])
                act(rs1, rs1, AF.Sqrt, [("rs1",)], [("rs1",)])
                recip(rs2, rs1, [("rs1",)], [("rs2",)])
                stt(ot[p], xr[:, t, :], rs2, gfb, ALU.mult, ALU.mult, rk + [("rs2",), ("gfb",)], [("ot", p)])
                dma("sp", out[t * 128:(t + 1) * 128, :], ot[p], f"ot{p}", [("ot", p)], [("out", t)])


def wo_load(dma, wv, WOUT, g):
    for q in range(4):
        dma("pool", wv[:, 4 * q:4 * q + 4, :], WOUT[:, 4 * q:4 * q + 4, g * 512:(g + 1) * 512],
            f"wo{g % 2}_{q}", (), [("wo", g % 2)])


def _t5_bucket(n):
    n = np.maximum(n, 0)
    max_exact = 16
    nf = np.maximum(n, 1).astype(np.float32)
    large = max_exact + (np.log(nf / max_exact) / math.log(128 / max_exact) * (32 - max_exact)).astype(np.int32)
    large = np.minimum(large, 31)
    return np.where(n < max_exact, n, large)


def _chunked(w, cols_list, kc):
    cols = np.concatenate(cols_list)
    n = len(cols_list)
    a = w[:, cols].reshape(kc, 128, n, 128).transpose(2, 1, 0, 3)
    return np.ascontiguousarray(a, dtype=np.float32)


_CACHE = {}


def kernel(x, mem, g_norm, w_in, g_cq, w_uq, g_ckv, w_ukv, g_mem, w_mem_kv, rel_bias,
           w_p_moba, w_p_mla, w_p_mem, w_out, g_final):
    f32 = np.float32
    x = np.asarray(x, f32)
    mem = np.asarray(mem, f32)
    w_in0 = np.asarray(w_in, f32)[0]
    ar = np.arange
    cl = []
    for h in range(8):
        cl += [1024 + h * 128 + ar(128), 2048 + h * 128 + ar(128), h * 128 + ar(128), 3072 + h * 128 + ar(128)]
    cl += [4608 + ar(128), 4608 + 128 + ar(128)]
    cl += [np.concatenate([4864 + ar(64), 4864 + 32 + ar(32), 4864 + ar(32)])]
    cl += [4096 + c * 128 + ar(128) for c in range(4)]
    cl += [4928 + h * 128 + ar(128) for h in range(8)]
    cl += [5952 + h * 128 + ar(128) for h in range(4)]
    cl += [6464 + h * 128 + ar(128) for h in range(4)]
    for br in range(3):
        cl += [6976 + br * 2048 + c * 128 + ar(128) for c in range(16)]
    assert len(cl) == N_WIN
    WIN = _chunked(w_in0, cl, 16)
    cl = []
    for h in range(8):
        cl += [h * 192 + ar(128), np.concatenate([h * 192 + 128 + ar(64), h * 192 + 128 + 32 + ar(32), h * 192 + 128 + ar(32)])]
    WUQ = _chunked(np.asarray(w_uq, f32)[0], cl, 4)
    cl = []
    for h in range(8):
        cl += [h * 256 + ar(128), h * 256 + 128 + ar(128)]
    WUKV = _chunked(np.asarray(w_ukv, f32)[0], cl, 2)
    WMKV = _chunked(np.asarray(w_mem_kv, f32)[0], [c * 128 + ar(128) for c in range(8)], 16)
    WPA = _chunked(np.asarray(w_p_moba, f32)[0], [c * 128 + ar(128) for c in range(16)], 8)
    WPB = _chunked(np.asarray(w_p_mla, f32)[0], [c * 128 + ar(128) for c in range(16)], 8)
    WPM = _chunked(np.asarray(w_p_mem, f32)[0], [c * 128 + ar(128) for c in range(16)], 4)
    WOUT = np.ascontiguousarray(np.asarray(w_out, f32)[0].reshape(16, 128, 2048).transpose(1, 0, 2))
    gcols = np.zeros((128, 40), f32)
    gcols[:, 0:16] = np.asarray(g_norm, f32)[0].reshape(16, 128).T
    gcols[:, 16:32] = np.asarray(g_mem, f32)[0].reshape(16, 128).T
    gcols[:, 32:36] = np.asarray(g_cq, f32)[0].reshape(4, 128).T
    gcols[:, 36:38] = np.asarray(g_ckv, f32)[0].reshape(2, 128).T
    gfin = np.asarray(g_final, f32).reshape(1, D)
    half = 32
    inv = (10000.0 ** (-np.arange(half, dtype=f32) / half)).astype(f32)
    i64 = np.arange(64)
    dd = np.arange(384) - 127
    E1 = np.zeros((33, 384), f32)
    bk = _t5_bucket(dd)
    for i in range(383):
        if dd[i] >= 0:
            E1[bk[i], i] += 1.0
            E1[31, i] -= 1.0
        else:
            E1[32, i] = 1.0
    rbaug = np.concatenate([np.asarray(rel_bias, f32), np.full((1, 8), NEG, f32)], axis=0)
    identb = np.eye(128, dtype=f32).astype(ml_dtypes.bfloat16)
    SELc = np.zeros((8, 8, 128), f32)
    for n in range(8):
        SELc[n, n, :] = 1.0
    SELc = SELc.reshape(8, 1024).astype(ml_dtypes.bfloat16)
    kk = np.arange(128)[:, None]
    qq = np.arange(128)[None, :]
    CAUSc = np.where(qq < kk, NEG, 0.0).astype(f32).astype(ml_dtypes.bfloat16)

    in_maps = []
    for c in range(NCORE):
        b, hf = c // 2, c % 2
        own = x[b, hf * 1024:(hf + 1) * 1024]
        ctx = x[b, 0:1024]
        xT = np.ascontiguousarray(np.concatenate([ctx, own], axis=0).T)
        pos = np.concatenate([np.arange(1024) + (hf - 1) * 1024, np.arange(1024) + hf * 1024]).astype(f32)
        ang = pos[None, :] * inv[i64 % 32][:, None]
        cosT = np.cos(ang).astype(f32)
        sn = np.sin(ang).astype(f32)
        ssinT = np.where((i64 < 32)[:, None], -sn, sn).astype(f32)
        pastb = np.zeros((128, 8, 8), f32)
        pasti = np.zeros((128, 8, 8), f32)
        npsel = np.zeros((128, 8, 8), f32)
        for i in range(8):
            ownblk = 4 + i // 2
            for n in range(8):
                past = (n < 4 and hf == 1) or (4 <= n < ownblk)
                pastb[:, i, n] = 0.0 if past else -1e30
                pasti[:, i, n] = 1.0 if past else 0.0
                npsel[:, i, n] = 0.0 if (past or n == ownblk) else NEG
        ctxb = np.full((128, 1), 0.0 if hf == 1 else NEG, f32)
        in_maps.append(dict(
            xT=xT, xo=np.ascontiguousarray(own), memT=np.ascontiguousarray(mem[b].T),
            WIN=WIN, WUQ=WUQ, WUKV=WUKV, WMKV=WMKV, WPA=WPA, WPB=WPB, WPM=WPM, WOUT=WOUT,
            gcols=gcols, gfin=gfin, cosT=cosT, ssinT=ssinT,
            pastb=pastb.reshape(128, 64), pasti=pasti.reshape(128, 64), npsel=npsel.reshape(128, 64),
            ctxb=ctxb, E1=E1, rbaug=rbaug, identb=identb, SELc=SELc, CAUSc=CAUSc))

    if "nc" not in _CACHE:
        _CACHE["nc"] = build_program()
    nc, _ = _CACHE["nc"]
    res = run_bass_kernel_spmd(nc, in_maps, core_ids=list(range(NCORE)))
    if DUMP:
        _CACHE["dump"] = [{k: np.asarray(res.results[c][k]) for k in ("dgA", "dgB", "dgM", "dyT", "dL", "dH")} for c in range(NCORE)]
    outp = np.zeros((B, S, D), f32)
    for c in range(NCORE):
        b, hf = c // 2, c % 2
        outp[b, hf * 1024:(hf + 1) * 1024] = res.results[c]["out"]
    return outp
```

```python
import contextlib
import math
import numpy as np
import ml_dtypes
import concourse.bass as bass
import concourse.mybir as mybir
from concourse.bass_utils import run_bass_kernel_spmd

F32 = mybir.dt.float32
BF16 = mybir.dt.bfloat16
AF = mybir.ActivationFunctionType
ALU = mybir.AluOpType
AX = mybir.AxisListType

D = 2048
S = 2048
B = 4
NCORE = 8
T_OWN = 1024
EPS = 1e-6
NEG = -30000.0
SC_MOBA = 128 ** -0.5
SC_MLA = 192 ** -0.5
SC_MEM = 128 ** -0.5

CH_MOBA = 0
CH_CKV = 32
CH_KR = 34
CH_CQ = 35
CH_ZB = 39
CH_QM = 47
CH_ZM = 51
CH_GL = 55
N_WIN = 103
STOP = 99
NH1 = 8
DBG = 99
LDBG = 99
DUMP = False


class Op:
    __slots__ = ("eng", "fn", "deps", "dsem", "signal", "count")

    def __init__(self, eng, fn, deps, dsem):
        self.eng = eng
        self.fn = fn
        self.deps = deps
        self.dsem = dsem
        self.signal = False
        self.count = 0


class Sched:
    ENGS = ("pe", "act", "dve", "pool", "sp")

    def __init__(self, nc, es):
        self.nc = nc
        self.es = es
        self.eng = dict(pe=nc.tensor, act=nc.scalar, dve=nc.vector, pool=nc.gpsimd, sp=nc.sync)
        self.ops = []
        self.lw = {}
        self.rd = {}
        self.last_on_eng = {}
        self.dma_pending = []

    def add(self, eng, fn, reads=(), writes=(), dsem=None):
        i = len(self.ops)
        deps = set()
        for k in reads:
            w = self.lw.get(k)
            if w is not None:
                deps.add(w)
            if k[0] == "ps":
                r = self.rd.get(k)
                if r:
                    deps.update(v for e_, v in r[0].items() if e_ != eng)
        for k in writes:
            w = self.lw.get(k)
            if w is not None:
                deps.add(w)
            r = self.rd.get(k)
            if r:
                deps.update(r[0].values())
                deps.update(r[1])
        for k in writes:
            self.lw[k] = i
            self.rd[k] = ({}, [])
        for k in reads:
            r = self.rd.get(k)
            if r is None:
                r = ({}, [])
                self.rd[k] = r
            if dsem is not None:
                r[1].append(i)
            else:
                r[0][eng] = i
        deps.discard(i)
        self.ops.append(Op(eng, fn, deps, dsem))
        self.last_on_eng[eng] = i
        if dsem is not None:
            self.dma_pending.append(i)
        return i

    def barrier(self):
        deps = set(self.last_on_eng.values()) | set(self.dma_pending)
        self.ops.append(Op("barrier", None, deps, None))
        self.dma_pending = []

    def emit(self):
        nc = self.nc
        ops = self.ops
        for op in ops:
            for d in op.deps:
                if op.eng == "pe" and ops[d].eng == "pe":
                    continue
                ops[d].signal = True
        cnt = {e: 0 for e in self.ENGS}
        dcnt = {}
        for op in ops:
            if op.eng == "barrier":
                continue
            if op.dsem is not None:
                dcnt[op.dsem] = dcnt.get(op.dsem, 0) + 16
                op.count = dcnt[op.dsem]
            elif op.signal:
                cnt[op.eng] += 1
                op.count = cnt[op.eng]
        sems = {}

        def sem(name):
            s = sems.get(name)
            if s is None:
                s = self.es.enter_context(nc.semaphore("s_" + name))
                sems[name] = s
            return s

        known = {e: {} for e in self.ENGS}

        def do_waits(e, deps):
            waits = {}
            for d in deps:
                dop = ops[d]
                if dop.dsem is not None:
                    nm = "d_" + dop.dsem
                else:
                    if dop.eng == "pe" and e == "pe":
                        continue
                    nm = "e_" + dop.eng
                if dop.count > waits.get(nm, 0):
                    waits[nm] = dop.count
            kn = known[e]
            for nm, val in waits.items():
                if kn.get(nm, 0) < val:
                    self.eng[e].wait_ge(sem(nm), val)
                    kn[nm] = val

        for op in ops:
            if op.eng == "barrier":
                for e in self.ENGS:
                    do_waits(e, op.deps)
                continue
            do_waits(op.eng, op.deps)
            ins = op.fn()
            if op.dsem is not None:
                ins.then_inc(sem("d_" + op.dsem), 16)
            elif op.signal:
                ins.then_inc(sem("e_" + op.eng), 1)
        self.stats = dict(n_ops=len(ops), cnt=cnt, n_sems=len(sems))


def build_program():
    nc = bass.Bass("TRN2", target_bir_lowering=False)

    def din(name, shape, dt=F32):
        return nc.dram_tensor(name, list(shape), dt, kind="ExternalInput").ap()

    xT = din("xT", [D, 2048])
    xo = din("xo", [T_OWN, D])
    memT = din("memT", [D, 256])
    WIN = din("WIN", [N_WIN, 128, 16, 128])
    WUQ = din("WUQ", [16, 128, 4, 128])
    WUKV = din("WUKV", [16, 128, 2, 128])
    WMKV = din("WMKV", [8, 128, 16, 128])
    WPA = din("WPA", [16, 128, 8, 128])
    WPB = din("WPB", [16, 128, 8, 128])
    WPM = din("WPM", [16, 128, 4, 128])
    WOUT = din("WOUT", [128, 16, 2048])
    gcols = din("gcols", [128, 40])
    gfin = din("gfin", [1, D])
    cosT = din("cosT", [64, 2048])
    ssinT = din("ssinT", [64, 2048])
    pastb = din("pastb", [128, 64])
    pasti = din("pasti", [128, 64])
    npsel = din("npsel", [128, 64])
    ctxb = din("ctxb", [128, 1])
    E1 = din("E1", [33, 384])
    rbaug = din("rbaug", [33, 8])
    identb = din("identb", [128, 128], BF16)
    SELc = din("SELc", [8, 8 * 128], BF16)
    CAUSc = din("CAUSc", [128, 128], BF16)
    out = nc.dram_tensor("out", [T_OWN, D], F32, kind="ExternalOutput").ap()
    if DUMP:
        dgA = nc.dram_tensor("dgA", [128, 8192], BF16, kind="ExternalOutput").ap()
        dgB = nc.dram_tensor("dgB", [128, 8192], BF16, kind="ExternalOutput").ap()
        dgM = nc.dram_tensor("dgM", [128, 4096], BF16, kind="ExternalOutput").ap()
        dyT = nc.dram_tensor("dyT", [128, 16384], BF16, kind="ExternalOutput").ap()
        dL = nc.dram_tensor("dL", [128, 10240], BF16, kind="ExternalOutput").ap()
        dH = nc.dram_tensor("dH", [128, 16384], BF16, kind="ExternalOutput").ap()
    scr = nc.dram_tensor("scr", [8, 128, 384], F32, kind="Internal")
    scr_ap = scr.ap()

    es = contextlib.ExitStack()
    with es:
        ARENA_KB = 196
        arena = es.enter_context(nc.sbuf_tensor("arena", [128, ARENA_KB * 512], BF16))
        banks = [es.enter_context(nc.psum_tensor(f"bank{i}", [128, 512], F32)) for i in range(8)]
        sch = Sched(nc, es)

        def vb(off_kb, nelem):
            o = int(off_kb * 512)
            return arena[:, o:o + nelem]

        def vf(off_kb, nelem):
            o = int(off_kb * 512)
            return arena[:, o:o + 2 * nelem].bitcast(F32)

        hTo = vb(0, 16 * 1024).rearrange("p (c t) -> p c t", c=16)
        hTc = vb(32, 16 * 1024).rearrange("p (c t) -> p c t", c=16)
        gA = vb(64, 8 * 1024).rearrange("p (c t) -> p c t", c=8)
        gB = vb(80, 8 * 1024).rearrange("p (c t) -> p c t", c=8)
        gM = vb(96, 4 * 1024).rearrange("p (c t) -> p c t", c=4)
        NSLOT = 4
        wsl = [vb(104 + 4 * i, 2048).rearrange("p (c n) -> p c n", c=16) for i in range(NSLOT)]
        ones_b = vb(120, 128)
        ident = vb(120.25, 128)
        caus = vb(120.5, 128)
        SEL = vb(120.75, 1024)
        Whi = vb(122.75, 8 * 256).rearrange("p (h x) -> p h x", h=8)
        Wlo = vb(126.75, 8 * 256).rearrange("p (h x) -> p h x", h=8)
        gc = vf(130.75, 40)
        ctxbias = vf(131, 1)
        zero_c = vf(131.25, 1)
        pastb_t = vf(131.5, 64)
        pasti_t = vf(131.75, 64)
        npsel_t = vf(132, 64)
        cos_own = vf(132.5, 1024)
        ssin_own = vf(136.5, 1024)
        TB = 140.5

        PSB = lambda i: ("ps", i)

        def mm(o, lhsT, rhs, start, stop, reads, writes):
            sch.add("pe", lambda: nc.tensor.matmul(o, lhsT=lhsT, rhs=rhs, start=start, stop=stop), reads, writes)

        def tr(o, in_, idn, reads, writes):
            sch.add("pe", lambda: nc.tensor.transpose(o, in_, idn), reads, writes)

        def act(o, in_, func, reads, writes, scale=1.0, bias=0.0, accum_out=None):
            if accum_out is None:
                sch.add("act", lambda: nc.scalar.activation(out=o, in_=in_, func=func, bias=bias, scale=scale), reads, writes)
            else:
                sch.add("act", lambda: nc.scalar.activation(out=o, in_=in_, func=func, bias=bias, scale=scale, accum_out=accum_out), reads, writes)

        def tt(o, a, b, op, reads, writes, eng="dve"):
            e = nc.vector if eng == "dve" else nc.gpsimd
            sch.add(eng, lambda: e.tensor_tensor(out=o, in0=a, in1=b, op=op), reads, writes)

        def ts(o, a, s1, s2, op0, op1, reads, writes):
            if op1 is None:
                sch.add("dve", lambda: nc.vector.tensor_scalar(out=o, in0=a, scalar1=s1, scalar2=None, op0=op0), reads, writes)
            else:
                sch.add("dve", lambda: nc.vector.tensor_scalar(out=o, in0=a, scalar1=s1, scalar2=s2, op0=op0, op1=op1), reads, writes)

        def stt(o, a, s, b, op0, op1, reads, writes):
            sch.add("dve", lambda: nc.vector.scalar_tensor_tensor(out=o, in0=a, scalar=s, in1=b, op0=op0, op1=op1), reads, writes)

        def cp(o, a, reads, writes):
            sch.add("dve", lambda: nc.vector.tensor_copy(out=o, in_=a), reads, writes)

        def recip(o, a, reads, writes):
            sch.add("dve", lambda: nc.vector.reciprocal(out=o, in_=a), reads, writes)

        def memset(o, val, writes):
            sch.add("dve", lambda: nc.vector.memset(o, val), (), writes)

        def dma(q, o, in_, dsem, reads, writes):
            e = {"sp": nc.sync, "pool": nc.gpsimd}[q]
            sch.add(q, lambda: e.dma_start(out=o, in_=in_), reads, writes, dsem=dsem)

        memset(ones_b, 1.0, [("ones",)])
        memset(zero_c, 0.0, [("zero",)])
        dma("sp", ident, identb, "c_ident", (), [("ident",)])
        dma("sp", caus, CAUSc, "c_caus", (), [("caus",)])
        memset(SEL, 0.0, [("SEL",)])
        dma("sp", SEL[0:8, :], SELc, "c_sel", (), [("SEL",)])
        dma("sp", gc, gcols, "c_gc", (), [("gc",)])
        dma("sp", ctxbias, ctxb, "c_ctxb", (), [("ctxb",)])
        dma("sp", pastb_t, pastb, "c_pb", (), [("pastb",)])
        dma("sp", pasti_t, pasti, "c_pi", (), [("pasti",)])
        dma("sp", npsel_t, npsel, "c_np", (), [("npsel",)])
        dma("sp", cos_own[0:64, :], cosT[:, 1024:2048], "c_cos", (), [("cos_own",)])
        dma("sp", ssin_own[0:64, :], ssinT[:, 1024:2048], "c_sin", (), [("ssin_own",)])

        ts(cos_own[0:64, :], cos_own[0:64, :], SC_MLA, None, ALU.mult, None, [("cos_own",)], [("cos_own",)])
        ts(ssin_own[0:64, :], ssin_own[0:64, :], SC_MLA, None, ALU.mult, None, [("ssin_own",)], [("ssin_own",)])

        wstate = {"n": 0}

        def wload(src_ap, kc):
            s = wstate["n"] % NSLOT
            wstate["n"] += 1
            key = ("w", s)
            dma("pool", wsl[s][:, 0:kc, :], src_ap, f"w{s}", (), [key])
            return wsl[s], key

        class WStream:
            def __init__(self, specs, look=3):
                self.specs = specs
                self.look = look
                self.loaded = []

            def get(self, i):
                while len(self.loaded) < min(len(self.specs), i + self.look):
                    src, kc = self.specs[len(self.loaded)]
                    self.loaded.append(wload(src, kc))
                return self.loaded[i]

        _sp = []
        for h in range(8):
            for r in range(4):
                _sp.append((WIN[CH_MOBA + 4 * h + r], 16))
        ws_B1 = WStream(_sp)
        ws_B2 = WStream([(WIN[CH_CKV], 16), (WIN[CH_CKV + 1], 16), (WIN[CH_KR], 16)] + [(WIN[CH_CQ + c], 16) for c in range(4)], look=3)
        _sp = []
        for h in range(8):
            _sp += [(WUKV[2 * h], 2), (WUKV[2 * h + 1], 2), (WUQ[2 * h], 4), (WUQ[2 * h + 1], 4), (WIN[CH_ZB + h], 16)]
        ws_B3 = WStream(_sp, look=4)
        _sp = []
        for hm in range(4):
            _sp += [(WMKV[hm], 16), (WMKV[4 + hm], 16), (WIN[CH_QM + hm], 16), (WIN[CH_ZM + hm], 16)]
        ws_B4 = WStream(_sp)
        _gw = [(WPA, 8), (WPB, 8), (WPM, 4)]
        _sp = []
        for c in range(16):
            for br in range(3):
                _sp += [(WIN[CH_GL + 16 * br + c], 16), (_gw[br][0][c], _gw[br][1])]
        ws_D = WStream(_sp)

        pbank = {"n": 0, "list": [4, 5, 6, 7]}

        def next_pbank():
            l = pbank["list"]
            b = l[pbank["n"] % len(l)]
            pbank["n"] += 1
            return b

        def proj(wv, wkey, kc_n, rhs_fn, evac, m_lo=0, m_hi=128, ncols=512):
            b = next_pbank()
            M = m_hi - m_lo
            o = banks[b][0:M, 0:ncols]
            for kc in range(kc_n):
                r_ap, r_key = rhs_fn(kc)
                mm(o, wv[:, kc, m_lo:m_hi], r_ap, kc == 0, kc == kc_n - 1, [wkey, r_key], [PSB(b)])
            evac(o, PSB(b))

        def hT(kc, tg):
            if tg < 2:
                return hTc[:, kc, tg * 512:(tg + 1) * 512], ("hTc", kc, tg)
            return hTo[:, kc, (tg - 2) * 512:(tg - 1) * 512], ("hTo", kc, tg - 2)

        def rms_stage(src, ntg, ncols, gcol0, dst_fn, TBo, alt_off=None):
            xs_sets = [[vf(TBo + 2 * c, 512) for c in range(16)]]
            xs_offs = [TBo]
            if alt_off is not None:
                xs_sets.append([vf(alt_off + 2 * c, 512) for c in range(16)])
                xs_offs.append(alt_off)
            rstd = vf(TBo + 32, 512)
            sqs = [vb(TBo + 34 + i, 512) for i in range(3)]
            tmp = vf(TBo + 37, 512)
            for tg in range(ntg):
                b = next_pbank()
                par = tg % len(xs_sets)
                xs = xs_sets[par]
                if False and ncols == 512:
                    base = xs_offs[par]
                    for cg in range(4):
                        dst4 = vf(base + 8 * cg, 4 * 512).rearrange("p (c t) -> p c t", c=4)
                        src4 = src[cg * 512:(cg + 1) * 512, tg * ncols:(tg + 1) * ncols].rearrange("(c p) t -> p c t", p=128)
                        dma("sp", dst4, src4, f"xs{par}_{cg}", (), [("xs", par, 4 * cg + i) for i in range(4)])
                for c in range(16):
                    if True:
                        dma("sp", xs[c][:, 0:ncols], src[c * 128:(c + 1) * 128, tg * ncols:(tg + 1) * ncols],
                            f"xs{par}_{c}", (), [("xs", par, c)])
                    sq = sqs[c % 3]
                    if DBG < 2:
                        continue
                    act(sq[:, 0:ncols], xs[c][:, 0:ncols], AF.Square, [("xs", par, c)], [("sq", c % 3)])
                    if DBG < 3:
                        continue
                    mm(banks[b][:, 0:ncols], ones_b, sq[:, 0:ncols], c == 0, c == 15,
                       [("ones",), ("sq", c % 3)], [PSB(b)])
                if DBG < 4:
                    continue
                ts(tmp[:, 0:ncols], banks[b][:, 0:ncols], 1.0 / D, EPS, ALU.mult, ALU.add, [PSB(b)], [("rtmp",)])
                act(tmp[:, 0:ncols], tmp[:, 0:ncols], AF.Sqrt, [("rtmp",)], [("rtmp",)])
                recip(rstd[:, 0:ncols], tmp[:, 0:ncols], [("rtmp",)], [("rstd",)])
                if DBG < 5:
                    continue
                for c in range(16):
                    d_ap, d_key = dst_fn(c, tg)
                    stt(d_ap, xs[c][:, 0:ncols], gc[:, gcol0 + c:gcol0 + c + 1], rstd[:, 0:ncols],
                        ALU.mult, ALU.mult, [("xs", par, c), ("gc",), ("rstd",)], [d_key])

        if STOP >= 1:
            rms_stage(xT, 4, 512, 0, hT, TB, alt_off=64)

        TW = TB + 40
        e1_t = vf(TW, 384)
        rb_t = vf(TW + 1.5, 8)
        rbrep = vf(TW + 1.75, 128)
        ones_f = vf(TW + 2.25, 128)
        vrep = vf(TW + 2.75, 384)
        wf = vf(TW + 4.5, 256)
        wd = vf(TW + 5.5, 256)
        dma("sp", e1_t[0:33, :], E1, "c_e1", (), [("e1",)])
        dma("sp", rb_t[0:33, :], rbaug, "c_rb", (), [("rb",)])
        memset(ones_f[0:33, :], 1.0, [("ones_f",)])
        for h in range(8 if STOP >= 2 else 0):
            ts(rbrep[0:33, :], ones_f[0:33, :], rb_t[0:33, h:h + 1], None, ALU.mult, None,
               [("ones_f",), ("rb",)], [("rbrep",)])
            b = next_pbank()
            mm(banks[b][:, 0:384], rbrep[0:33, :], e1_t[0:33, :], True, True, [("rbrep",), ("e1",)], [PSB(b)])
            cp(vrep, banks[b][:, 0:384], [PSB(b)], [("vrep",)])
            dma("sp", scr_ap[h], vrep, "scrw", [("vrep",)], [("scr", h)])
            skew = bass.AP(scr, h * 128 * 384 + 127, [[383, 128], [1, 256]])
            dma("sp", wf, skew, "scrr", [("scr", h)], [("wf",)])
            cp(Whi[:, h, :], wf, [("wf",)], [("Whi", h)])
            tt(wd, wf, Whi[:, h, :], ALU.subtract, [("wf",), ("Whi", h)], [("wd",)])
            cp(Wlo[:, h, :], wd, [("wd",)], [("Wlo", h)])

        ws_B1.get(0)
        sch.barrier()

        PT = [vb(TB + i, 512) for i in range(4)]
        rden = [vf(TB + 4, 512), vf(TB + 190.5 - 140.5, 512)]
        otmp = [vf(TB + 6, 512), vf(TB + 192.5 - 140.5, 512)]
        att = {"n": 0, "pair": 0, "S": [0, 1], "LA": 1, "pairs": [(2, 3), (4, 5)]}

        def attention(n_ktiles_fn, logit_mms, exp_bias_fn, v_fn, dst_fn, siluz_fn, col0_fn):
            LA = att["LA"]
            Sb = att["S"]
            for G in range(2):
                ob, db = att["pairs"][att["pair"] % 2]
                att["pair"] += 1
                nk = n_ktiles_fn(G)
                info = {}
                for jj in range(nk + LA):
                    if jj < nk:
                        j = jj
                        c0 = col0_fn(G, j)
                        N = 512 - c0
                        sb = Sb[att["n"] % len(Sb)]
                        slot = att["n"] % 4
                        att["n"] += 1
                        logit_mms(G, j, c0, banks[sb][:, 0:N], PSB(sb))
                        bias, bkey = exp_bias_fn(j)
                        act(PT[slot][:, 0:N], banks[sb][:, 0:N], AF.Exp, [PSB(sb)] + ([bkey] if bkey else []),
                            [("PT", slot)], bias=bias)
                        info[j] = (c0, N, slot)
                    if jj >= LA:
                        j = jj - LA
                        c0, N, slot = info[j]
                        v_ap, v_key = v_fn(j)
                        mm(banks[ob][:, c0:512], v_ap, PT[slot][:, 0:N], j == 0, j == nk - 1, [v_key, ("PT", slot)], [PSB(ob)])
                        mm(banks[db][:, c0:512], ones_b, PT[slot][:, 0:N], j == 0, j == nk - 1, [("ones",), ("PT", slot)], [PSB(db)])
                    yield
                pr = att["pair"] % 2
                recip(rden[pr], banks[db][:, :], [PSB(db)], [("rden", pr)])
                tt(otmp[pr], banks[ob][:, :], rden[pr], ALU.mult, [PSB(ob), ("rden", pr)], [("otmp", pr)])
                d_ap, d_key = dst_fn(G)
                z_ap, z_key = siluz_fn(G)
                tt(d_ap, otmp[pr], z_ap, ALU.mult, [("otmp", pr), z_key], [d_key])

        def interleave(ga, gp, na=2, npj=1):
            a_done = ga is None
            p_done = gp is None
            while not (a_done and p_done):
                for _ in range(na):
                    if not a_done:
                        try:
                            next(ga)
                        except StopIteration:
                            a_done = True
                for _ in range(npj):
                    if not p_done:
                        try:
                            next(gp)
                        except StopIteration:
                            p_done = True

        def transpose_v(vT, vT_key, ntiles, dst_fn):
            for j4 in range(0, ntiles, 4):
                n = min(4, ntiles - j4)
                b = next_pbank()
                pb = banks[b][:, :].bitcast(BF16)
                for jj in range(n):
                    tr(pb[:, jj * 128:(jj + 1) * 128], vT[:, (j4 + jj) * 128:(j4 + jj + 1) * 128], ident,
                       [vT_key, ("ident",)], [PSB(b)])
                d_ap, d_key = dst_fn(j4, n)
                cp(d_ap, pb[:, 0:n * 128], [PSB(b)], [d_key])

        MB = TB + 8
        kT = [vb(MB + 4 * i, 2048) for i in range(2)]
        vtok = [vb(MB + 8 + 4 * i, 2048) for i in range(2)]
        qT = [vb(MB + 16 + 2 * i, 1024) for i in range(2)]
        szT = [vb(MB + 20 + 2 * i, 1024) for i in range(2)]
        vT_tmp = vb(MB + 24, 2048)
        selbT = [vb(MB + 28 + 2 * i, 1024) for i in range(2)]
        ksum = vf(MB + 32, 8)
        kmean_b = vb(MB + 32.25, 8)
        gm = vf(MB + 32.5, 64)
        mx8 = vf(MB + 33, 8)
        selb1 = vf(MB + 33.25, 64)
        selb2 = vf(MB + 33.75, 64)
        selb3 = vb(MB + 34.25, 64)

        pbank["list"] = [6, 7]
        for i in range(2):
            memset(selbT[i], 0.0, [("selbT", i)])
        specs = []
        for h in range(8):
            for r in range(4):
                specs.append((WIN[CH_MOBA + 4 * h + r], 16))
        ws = ws_B1
        def b1_proj(h):
            p = h % 2
            kT_h, vt_h, qT_h, sz_h, sb_h = kT[p], vtok[p], qT[p], szT[p], selbT[p]
            wv, wk = ws.get(4 * h + 0)
            for tg in range(4):
                def ev_k(o, ok, tg=tg):
                    for hh in range(2):
                        act(kT_h[:, tg * 512 + hh * 256: tg * 512 + (hh + 1) * 256], o[:, hh * 256:(hh + 1) * 256], AF.Copy,
                            [ok], [("kT", p, tg), ("ksum", tg)], accum_out=ksum[:, 2 * tg + hh:2 * tg + hh + 1])
                proj(wv, wk, 16, lambda kc, tg=tg: hT(kc, tg), ev_k)
                yield
            ts(kmean_b, ksum, 1.0 / 256, None, ALU.mult, None, [("ksum", t) for t in range(4)], [("kmean",)])
            wv, wk = ws.get(4 * h + 1)
            for tg in range(4):
                def ev_v(o, ok, tg=tg):
                    cp(vT_tmp[:, tg * 512:(tg + 1) * 512], o, [ok], [("vT", tg)])
                proj(wv, wk, 16, lambda kc, tg=tg: hT(kc, tg), ev_v)
                yield
            for tg in range(4):
                transpose_v(vT_tmp[:, tg * 512:(tg + 1) * 512], ("vT", tg), 4,
                            lambda j4, n, tg=tg: (vt_h[:, (tg * 4) * 128:(tg * 4 + 4) * 128], ("vtok", p, tg)))
                yield
            wv, wk = ws.get(4 * h + 2)
            for tg in range(2):
                def ev_q(o, ok, tg=tg):
                    act(qT_h[:, tg * 512:(tg + 1) * 512], o, AF.Copy, [ok], [("qT", p, tg)], scale=SC_MOBA)
                proj(wv, wk, 16, lambda kc, tg=tg: hT(kc, tg + 2), ev_q)
                yield
            wv, wk = ws.get(4 * h + 3)
            for tg in range(2):
                def ev_z(o, ok, tg=tg):
                    act(sz_h[:, tg * 512:(tg + 1) * 512], o, AF.Silu, [ok], [("szT", p, tg)])
                proj(wv, wk, 16, lambda kc, tg=tg: hT(kc, tg + 2), ev_z)
                yield
            b = next_pbank()
            for i in range(8):
                mm(banks[b][:, i * 8:(i + 1) * 8], qT_h[:, i * 128:(i + 1) * 128], kmean_b, True, True,
                   [("qT", p, i // 4), ("kmean",)], [PSB(b)])
            tt(gm, banks[b][:, 0:64], pastb_t, ALU.add, [PSB(b), ("pastb",)], [("gm",)])
            for i in range(8):
                sl = slice(i * 8, (i + 1) * 8)
                sch.add("dve", lambda sl=sl: nc.vector.max(out=mx8, in_=gm[:, sl]), [("gm",)], [("mx8",)])
                ts(selb1[:, sl], gm[:, sl], mx8[:, 2:3], NEG, ALU.is_lt, ALU.mult, [("gm",), ("mx8",)], [("selb1", i)])
            tt(selb2, selb1, pasti_t, ALU.mult, [("selb1", i) for i in range(8)] + [("pasti",)], [("selb2",)])
            tt(selb3, selb2, npsel_t, ALU.add, [("selb2",), ("npsel",)], [("selb3",)])
            yield
            b = next_pbank()
            pb = banks[b][:, :].bitcast(BF16)
            for i in range(8):
                tr(pb[0:8, i * 128:(i + 1) * 128], selb3[:, i * 8:(i + 1) * 8], ident, [("selb3",), ("ident",)], [PSB(b)])
            cp(sb_h[0:8, :], pb[0:8, 0:1024], [PSB(b)], [("selbT", p)])
            yield

        def b1_attn(h):
            p = h % 2
            kT_h, vt_h, qT_h, sz_h, sb_h = kT[p], vtok[p], qT[p], szT[p], selbT[p]

            def lm(G, j, c0, o, ok):
                N = 512 - c0
                q0 = 512 * G + c0
                n = j // 2
                ip = j - 8
                extra = []
                for (qt, xo_) in ((ip, 0), (ip + 1, 128)):
                    if qt < 0 or qt > 7:
                        continue
                    cc = qt * 128 - 512 * G
                    if cc < c0 or cc >= 512:
                        continue
                    extra.append((cc - c0, xo_))
                mm(o, kT_h[:, j * 128:(j + 1) * 128], qT_h[:, q0:q0 + N], True, False,
                   [("kT", p, j // 4), ("qT", p, G)], [ok])
                for (cc, xo_) in extra:
                    mm(o[:, cc:cc + 128], ident, Whi[:, h, xo_:xo_ + 128], False, False, [("ident",), ("Whi", h)], [ok])
                    mm(o[:, cc:cc + 128], ident, Wlo[:, h, xo_:xo_ + 128], False, False, [("ident",), ("Wlo", h)], [ok])
                mm(o, SEL[:, n * 128:(n + 1) * 128], sb_h[:, q0:q0 + N], False, True,
                   [("SEL",), ("selbT", p)], [ok])

            yield from attention(lambda G: 8 + 4 * G + 4, lm, lambda j: (0.0, None),
                                 lambda j: (vt_h[:, j * 128:(j + 1) * 128], ("vtok", p, j // 4)),
                                 lambda G: (gA[:, h, G * 512:(G + 1) * 512], ("gA", h, G)),
                                 lambda G: (sz_h[:, G * 512:(G + 1) * 512], ("szT", p, G)),
                                 lambda G, j: max(0, j - 8 - 4 * G) * 128)

        nh1 = NH1 if STOP >= 3 else 0
        if nh1 > 0:
            for _ in b1_proj(0):
                pass
        for h in range(nh1):
            interleave(b1_attn(h), b1_proj(h + 1) if h + 1 < nh1 else None, 3, 2)

        ws_B2.get(0)
        sch.barrier()

        pbank["list"] = [4, 5, 6, 7]
        LB = TB + 8
        ckvn = vb(LB, 2 * 2048).rearrange("p (c t) -> p c t", c=2)
        cqn = vb(LB + 8, 4 * 1024).rearrange("p (c t) -> p c t", c=4)
        kropeT = vb(LB + 16, 2048)
        sq2 = [vb(LB + 20 + i, 512) for i in range(2)]
        rtmp2 = vf(LB + 22, 512)
        rstd2 = vf(LB + 24, 512)
        cs_t = vf(LB + 26, 512)
        sn_t = vf(LB + 28, 512)
        t1 = vf(LB + 30, 512)
        t2 = vf(LB + 32, 512)

        memset(kropeT, 0.0, [("kropeT", t) for t in range(4)])
        specs = [(WIN[CH_CKV], 16), (WIN[CH_CKV + 1], 16), (WIN[CH_KR], 16)] + [(WIN[CH_CQ + c], 16) for c in range(4)]
        ws = ws_B2

        def latent(nch, ws_i0, tgs, tg_off, gcol0, dst, dst_name, dim):
            for tg in tgs:
                sb_ = 1
                for c in range(nch):
                    wv, wk = ws.get(ws_i0 + c)

                    def ev(o, ok, c=c, tg=tg):
                        ts(dst[:, c, tg * 512:(tg + 1) * 512], o, gc[:, gcol0 + c:gcol0 + c + 1], None, ALU.mult, None,
                           [ok, ("gc",)], [(dst_name, c, tg)])
                        if LDBG < 1:
                            return
                        act(sq2[c % 2], o, AF.Square, [ok], [("sq2", c % 2)])
                        if LDBG < 2:
                            return
                        mm(banks[sb_][:, :], ones_b, sq2[c % 2], c == 0, c == nch - 1, [("ones",), ("sq2", c % 2)], [PSB(sb_)])
                    proj(wv, wk, 16, lambda kc, tg=tg: hT(kc, tg + tg_off), ev)
                if LDBG < 3:
                    continue
                ts(rtmp2, banks[sb_][:, :], 1.0 / dim, EPS, ALU.mult, ALU.add, [PSB(sb_)], [("rtmp2",)])
                act(rtmp2, rtmp2, AF.Sqrt, [("rtmp2",)], [("rtmp2",)])
                recip(rstd2, rtmp2, [("rtmp2",)], [("rstd2",)])
                if LDBG < 4:
                    continue
                for c in range(nch):
                    tt(dst[:, c, tg * 512:(tg + 1) * 512], dst[:, c, tg * 512:(tg + 1) * 512], rstd2, ALU.mult,
                       [(dst_name, c, tg), ("rstd2",)], [(dst_name, c, tg)])

        if STOP < 4:
            sch.barrier()
            sch.emit()
            return nc, sch
        latent(2, 0, range(4), 0, 36, ckvn, "ckvn", 256)

        def rope_pair(wv, wk, kc_n, rhs_fn, cos_ap, cos_key, sin_ap, sin_key, dst_ap, dst_key, scale):
            hold = {}

            def evA(o, ok):
                tt(t1[0:64, :], o, cos_ap, ALU.mult, [ok, cos_key], [("t1",)])

            def evB(o, ok):
                tt(t2[0:64, :], o, sin_ap, ALU.mult, [ok, sin_key], [("t2",)])
            proj(wv, wk, kc_n, rhs_fn, evA, 0, 64)
            proj(wv, wk, kc_n, rhs_fn, evB, 64, 128)
            stt(dst_ap, t1[0:64, :], scale, t2[0:64, :], ALU.mult, ALU.add, [("t1",), ("t2",)], [dst_key])

        wv, wk = ws.get(2)
        for tg in range(4 if DBG >= 21 else 0):
            dma("sp", cs_t[0:64, :], cosT[:, tg * 512:(tg + 1) * 512], "cs_t", (), [("cs_t",)])
            dma("sp", sn_t[0:64, :], ssinT[:, tg * 512:(tg + 1) * 512], "sn_t", (), [("sn_t",)])
            rope_pair(wv, wk, 16, lambda kc, tg=tg: hT(kc, tg), cs_t[0:64, :], ("cs_t",), sn_t[0:64, :], ("sn_t",),
                      kropeT[0:64, tg * 512:(tg + 1) * 512], ("kropeT", tg), 1.0)

        if DBG >= 22:
            latent(4, 3, range(2), 2, 32, cqn, "cqn", 512)

        ws_B3.get(0)
        sch.barrier()

        pbank["list"] = [6, 7]
        att.update(S=[0, 1], LA=1, pairs=[(2, 3), (4, 5)])
        HB = 32
        knT = [vb(HB + 4 * i, 2048) for i in range(2)]
        vtb = [vb(HB + 8 + 4 * i, 2048) for i in range(2)]
        qnT = [vb(HB + 16 + 2 * i, 1024) for i in range(2)]
        qrT = [vb(HB + 20 + 2 * i, 1024) for i in range(2)]
        szB = [vb(HB + 24 + 2 * i, 1024) for i in range(2)]
        vT2 = vb(HB + 28, 2048)

        for i in range(2):
            memset(qrT[i], 0.0, [("qrT", i, 0), ("qrT", i, 1)])
        specs = []
        for h in range(8):
            specs += [(WUKV[2 * h], 2), (WUKV[2 * h + 1], 2), (WUQ[2 * h], 4), (WUQ[2 * h + 1], 4), (WIN[CH_ZB + h], 16)]
        ws = ws_B3
        if STOP < 5:
            sch.barrier()
            sch.emit()
            return nc, sch
        def b3_proj(h):
            p = h % 2
            kn_h, vt_h, qn_h, qr_h, sz_h = knT[p], vtb[p], qnT[p], qrT[p], szB[p]
            wv, wk = ws.get(5 * h + 0)
            for tg in range(4):
                def ev_k(o, ok, tg=tg):
                    act(kn_h[:, tg * 512:(tg + 1) * 512], o, AF.Copy, [ok], [("knT", p, tg)])
                proj(wv, wk, 2, lambda kc, tg=tg: (ckvn[:, kc, tg * 512:(tg + 1) * 512], ("ckvn", kc, tg)), ev_k)
                yield
            wv, wk = ws.get(5 * h + 1)
            for tg in range(4):
                def ev_v(o, ok, tg=tg):
                    cp(vT2[:, tg * 512:(tg + 1) * 512], o, [ok], [("vT2", tg)])
                proj(wv, wk, 2, lambda kc, tg=tg: (ckvn[:, kc, tg * 512:(tg + 1) * 512], ("ckvn", kc, tg)), ev_v)
                yield
            for tg in range(4):
                transpose_v(vT2[:, tg * 512:(tg + 1) * 512], ("vT2", tg), 4,
                            lambda j4, n, tg=tg: (vt_h[:, (tg * 4) * 128:(tg * 4 + 4) * 128], ("vtb", p, tg)))
                yield
            wv, wk = ws.get(5 * h + 2)
            for tg in range(2):
                def ev_q(o, ok, tg=tg):
                    act(qn_h[:, tg * 512:(tg + 1) * 512], o, AF.Copy, [ok], [("qnT", p, tg)], scale=SC_MLA)
                proj(wv, wk, 4, lambda kc, tg=tg: (cqn[:, kc, tg * 512:(tg + 1) * 512], ("cqn", kc, tg)), ev_q)
                yield
            wv, wk = ws.get(5 * h + 3)
            for tg in range(2):
                rope_pair(wv, wk, 4, lambda kc, tg=tg: (cqn[:, kc, tg * 512:(tg + 1) * 512], ("cqn", kc, tg)),
                          cos_own[0:64, tg * 512:(tg + 1) * 512], ("cos_own",),
                          ssin_own[0:64, tg * 512:(tg + 1) * 512], ("ssin_own",),
                          qr_h[0:64, tg * 512:(tg + 1) * 512], ("qrT", p, tg), 1.0)
                yield
            wv, wk = ws.get(5 * h + 4)
            for tg in range(2):
                def ev_z(o, ok, tg=tg):
                    act(sz_h[:, tg * 512:(tg + 1) * 512], o, AF.Silu, [ok], [("szB", p, tg)])
                proj(wv, wk, 16, lambda kc, tg=tg: hT(kc, tg + 2), ev_z)
                yield

        def b3_attn(h):
            p = h % 2
            kn_h, vt_h, qn_h, qr_h, sz_h = knT[p], vtb[p], qnT[p], qrT[p], szB[p]

            def lm(G, j, c0, o, ok):
                N = 512 - c0
                q0 = 512 * G + c0
                diag = (j >= 8 and 0 <= (j - 8) * 128 - 512 * G < 512)
                mm(o, kn_h[:, j * 128:(j + 1) * 128], qn_h[:, q0:q0 + N], True, False,
                   [("knT", p, j // 4), ("qnT", p, G)], [ok])
                if diag:
                    cc = (j - 8) * 128 - 512 * G - c0
                    mm(o[:, cc:cc + 128], ident, caus, False, False, [("ident",), ("caus",)], [ok])
                mm(o, kropeT[:, j * 128:(j + 1) * 128], qr_h[:, q0:q0 + N], False, True,
                   [("kropeT", j // 4), ("qrT", p, G)], [ok])

            yield from attention(lambda G: 8 + 4 * G + 4, lm,
                                 lambda j: ((ctxbias, ("ctxb",)) if j < 8 else (0.0, None)),
                                 lambda j: (vt_h[:, j * 128:(j + 1) * 128], ("vtb", p, j // 4)),
                                 lambda G: (gB[:, h, G * 512:(G + 1) * 512], ("gB", h, G)),
                                 lambda G: (sz_h[:, G * 512:(G + 1) * 512], ("szB", p, G)),
                                 lambda G, j: max(0, j - 8 - 4 * G) * 128)

        for _ in b3_proj(0):
            pass
        for h in range(8):
            interleave(b3_attn(h), b3_proj(h + 1) if h + 1 < 8 else None, 3, 2)

        ws_B4.get(0)
        sch.barrier()
        if DUMP:
            dma("sp", dL, vb(LB, 10240), "dump4", (), [("dump", 4)])
            dma("sp", dH, vb(HB, 16384), "dump5", (), [("dump", 5)])
            sch.barrier()

        mT = vb(HB, 16 * 256).rearrange("p (c t) -> p c t", c=16)
        kmT = vb(HB + 8, 4 * 256).rearrange("p (c t) -> p c t", c=4)
        vmtok = vb(HB + 10, 2 * 512)
        qmT = vb(HB + 12, 1024)
        szM = vb(HB + 14, 1024)
        vmT = vb(HB + 16, 256)

        if STOP < 6:
            sch.barrier()
            sch.emit()
            return nc, sch

        def mT_dst(c, tg):
            return mT[:, c, :], ("mT", c)
        rms_stage(memT, 1, 256, 16, mT_dst, TB + 8)

        specs = []
        for hm in range(4):
            specs += [(WMKV[hm], 16), (WMKV[4 + hm], 16), (WIN[CH_QM + hm], 16), (WIN[CH_ZM + hm], 16)]
        ws = ws_B4
        for hm in range(4):
            wv, wk = ws.get(4 * hm + 0)

            def ev_k(o, ok, hm=hm):
                act(kmT[:, hm, :], o, AF.Copy, [ok], [("kmT", hm)])
            proj(wv, wk, 16, lambda kc: (mT[:, kc, :], ("mT", kc)), ev_k, ncols=256)
            wv, wk = ws.get(4 * hm + 1)

            def ev_v(o, ok):
                cp(vmT, o, [ok], [("vmT",)])
            proj(wv, wk, 16, lambda kc: (mT[:, kc, :], ("mT", kc)), ev_v, ncols=256)
            b = next_pbank()
            pb = banks[b][:, :].bitcast(BF16)
            for jj in range(2):
                tr(pb[:, jj * 128:(jj + 1) * 128], vmT[:, jj * 128:(jj + 1) * 128], ident, [("vmT",), ("ident",)], [PSB(b)])
            for jj in range(2):
                cp(vmtok[:, jj * 512 + hm * 128: jj * 512 + (hm + 1) * 128], pb[:, jj * 128:(jj + 1) * 128],
                   [PSB(b)], [("vmtok", jj, hm)])
            wv, wk = ws.get(4 * hm + 2)
            for tg in range(2):
                def ev_q(o, ok, tg=tg):
                    act(qmT[:, tg * 512:(tg + 1) * 512], o, AF.Copy, [ok], [("qmT", tg)], scale=SC_MEM)
                proj(wv, wk, 16, lambda kc, tg=tg: hT(kc, tg + 2), ev_q)
            wv, wk = ws.get(4 * hm + 3)
            for tg in range(2):
                def ev_z(o, ok, tg=tg):
                    act(szM[:, tg * 512:(tg + 1) * 512], o, AF.Silu, [ok], [("szM", tg)])
                proj(wv, wk, 16, lambda kc, tg=tg: hT(kc, tg + 2), ev_z)

            def lm(G, j, c0, o, ok, hm=hm):
                mm(o, kmT[:, hm, j * 128:(j + 1) * 128], qmT[:, 512 * G:512 * G + 512], True, True,
                   [("kmT", hm), ("qmT", G)], [ok])

            for _ in attention(lambda G: 2, lm, lambda j: (0.0, None),
                      lambda j, hm=hm: (vmtok[:, j * 512 + hm * 128: j * 512 + (hm + 1) * 128], ("vmtok", j, hm)),
                      lambda G, hm=hm: (gM[:, hm, G * 512:(G + 1) * 512], ("gM", hm, G)),
                      lambda G: (szM[:, G * 512:(G + 1) * 512], ("szM", G)),
                      lambda G, j: 0):
                pass

        ws_D.get(0)
        sch.barrier()

        if STOP < 7:
            sch.barrier()
            sch.emit()
            return nc, sch
        pbank["list"] = [0, 1, 2, 3, 4, 5, 6, 7]
        yT = vb(TB + 8, 16 * 1024).rearrange("p (c t) -> p c t", c=16)
        sig = [vf(32 + 2 * i, 512) for i in range(3)]
        yacc0 = [vf(38 + 2 * i, 512) for i in range(2)]
        yacc1 = [vf(42 + 2 * i, 512) for i in range(2)]
        ytmp = [vf(TB + 2 * i, 512) for i in range(2)]
        wo_slots = [vb(46, 16 * 512).rearrange("p (c n) -> p c n", c=16), vb(62, 16 * 512).rearrange("p (c n) -> p c n", c=16)]
        gsrc = [(gA, "gA", 8, WPA), (gB, "gB", 8, WPB), (gM, "gM", 4, WPM)]
        specs = []
        for c in range(16):
            for br in range(3):
                specs += [(WIN[CH_GL + 16 * br + c], 16), (gsrc[br][3][c], gsrc[br][2])]
        ws = ws_D
        for c in range(16):
            if c == 12:
                wo_load(dma, wo_slots[0], WOUT, 0)
            for br in range(3):
                gten, gname, gk, _ = gsrc[br]
                wvg, wkg = ws.get((c * 3 + br) * 2)
                wvp, wkp = ws.get((c * 3 + br) * 2 + 1)
                for G in range(2):
                    def ev_gl(o, ok, br=br):
                        act(sig[br], o, AF.Sigmoid, [ok], [("sig", br)])
                    proj(wvg, wkg, 16, lambda kc, G=G: hT(kc, G + 2), ev_gl)

                    def ev_p(o, ok, br=br, G=G, c=c):
                        if br == 0:
                            tt(yacc0[G], o, sig[0], ALU.mult, [ok, ("sig", 0)], [("yacc0", G)])
                        elif br == 1:
                            tt(ytmp[0], o, sig[1], ALU.mult, [ok, ("sig", 1)], [("ytmp", 0)])
                            tt(yacc1[G], yacc0[G], ytmp[0], ALU.add, [("yacc0", G), ("ytmp", 0)], [("yacc1", G)])
                        else:
                            tt(ytmp[1], o, sig[2], ALU.mult, [ok, ("sig", 2)], [("ytmp", 1)])
                            tt(yT[:, c, G * 512:(G + 1) * 512], yacc1[G], ytmp[1], ALU.add,
                               [("yacc1", G), ("ytmp", 1)], [("yT", c, G)])
                    proj(wvp, wkp, gk, lambda kc, G=G, gten=gten, gname=gname: (gten[:, kc, G * 512:(G + 1) * 512], (gname, kc, G)), ev_p)

        sch.barrier()
        if DUMP:
            dma("sp", dgA, vb(64, 8192), "dump0", (), [("dump", 0)])
            dma("sp", dgB, vb(80, 8192), "dump1", (), [("dump", 1)])
            dma("sp", dgM, vb(96, 4096), "dump2", (), [("dump", 2)])
            dma("sp", dyT, vb(TB + 8, 16384), "dump3", (), [("dump", 3)])
            sch.barrier()
        if STOP < 8:
            sch.emit()
            return nc, sch
        return_stage_e(nc, sch, vb, vf, banks, yT, WOUT, gfin, xo, out, act, tt, stt, recip, mm, dma, PSB, wo_slots)
        sch.barrier()
        sch.emit()
    return nc, sch


def return_stage_e(nc, sch, vb, vf, banks, yT, WOUT, gfin, xo, out, act, tt, stt, recip, mm, dma, PSB, wo_slots):
    xr = vf(80, 8 * 2048).rearrange("p (t n) -> p t n", t=8)
    gfb = vf(0, 2048)
    ot = [vf(8 + 8 * i, 2048) for i in range(2)]
    sqj = vb(24, 2048)
    ssum = vf(28, 1)
    rs1 = vf(28.25, 1)
    rs2 = vf(28.5, 1)
    dma("sp", gfb, gfin.partition_broadcast(128), "c_gf", (), [("gfb",)])
    for t in range(8):
        dma("sp", xr[:, t, :], xo[t * 128:(t + 1) * 128, :], f"xr{t}", (), [("xr", t, g) for g in range(4)])
    nb = 0
    for g in range(4):
        wv = wo_slots[g % 2]
        if g >= 1:
            wo_load(dma, wv, WOUT, g)
        if g + 1 < 4 and g + 1 >= 2:
            pass
        for t in range(8):
            b = nb % 4
            nb += 1
            for kc in range(16):
                mm(banks[b][:, :], yT[:, kc, t * 128:(t + 1) * 128], wv[:, kc, :],
                   kc == 0, kc == 15, [("yT", kc, t // 4), ("wo", g % 2)], [PSB(b)])
            tt(xr[:, t, g * 512:(g + 1) * 512], banks[b][:, :], xr[:, t, g * 512:(g + 1) * 512], ALU.add,
               [PSB(b), ("xr", t, g)], [("xr", t, g)])
            if g == 3:
                p = t % 2
                rk = [("xr", t, gg) for gg in range(4)]
                act(sqj, xr[:, t, :], AF.Square, rk, [("sqj",), ("ssum",)], accum_out=ssum)
                sch.add("dve", lambda: nc.vector.tensor_scalar(out=rs1, in0=ssum, scalar1=1.0 / D, scalar2=EPS, op0=ALU.mult, op1=ALU.add),
                        [("ssum",)], [("rs1",)])
                act(rs1, rs1, AF.Sqrt, [("rs1",)], [("rs1",)])
                recip(rs2, rs1, [("rs1",)], [("rs2",)])
                stt(ot[p], xr[:, t, :], rs2, gfb, ALU.mult, ALU.mult, rk + [("rs2",), ("gfb",)], [("ot", p)])
                dma("sp", out[t * 128:(t + 1) * 128, :], ot[p], f"ot{p}", [("ot", p)], [("out", t)])


def wo_load(dma, wv, WOUT, g):
    for q in range(4):
        dma("pool", wv[:, 4 * q:4 * q + 4, :], WOUT[:, 4 * q:4 * q + 4, g * 512:(g + 1) * 512],
            f"wo{g % 2}_{q}", (), [("wo", g % 2)])


def _t5_bucket(n):
    n = np.maximum(n, 0)
    max_exact = 16
    nf = np.maximum(n, 1).astype(np.float32)
    large = max_exact + (np.log(nf / max_exact) / math.log(128 / max_exact) * (32 - max_exact)).astype(np.int32)
    large = np.minimum(large, 31)
    return np.where(n < max_exact, n, large)


def _chunked(w, cols_list, kc):
    cols = np.concatenate(cols_list)
    n = len(cols_list)
    a = w[:, cols].reshape(kc, 128, n, 128).transpose(2, 1, 0, 3)
    return np.ascontiguousarray(a, dtype=np.float32)


_CACHE = {}


def kernel(x, mem, g_norm, w_in, g_cq, w_uq, g_ckv, w_ukv, g_mem, w_mem_kv, rel_bias,
           w_p_moba, w_p_mla, w_p_mem, w_out, g_final):
    f32 = np.float32
    x = np.asarray(x, f32)
    mem = np.asarray(mem, f32)
    w_in0 = np.asarray(w_in, f32)[0]
    ar = np.arange
    cl = []
    for h in range(8):
        cl += [1024 + h * 128 + ar(128), 2048 + h * 128 + ar(128), h * 128 + ar(128), 3072 + h * 128 + ar(128)]
    cl += [4608 + ar(128), 4608 + 128 + ar(128)]
    cl += [np.concatenate([4864 + ar(64), 4864 + 32 + ar(32), 4864 + ar(32)])]
    cl += [4096 + c * 128 + ar(128) for c in range(4)]
    cl += [4928 + h * 128 + ar(128) for h in range(8)]
    cl += [5952 + h * 128 + ar(128) for h in range(4)]
    cl += [6464 + h * 128 + ar(128) for h in range(4)]
    for br in range(3):
        cl += [6976 + br * 2048 + c * 128 + ar(128) for c in range(16)]
    assert len(cl) == N_WIN
    WIN = _chunked(w_in0, cl, 16)
    cl = []
    for h in range(8):
        cl += [h * 192 + ar(128), np.concatenate([h * 192 + 128 + ar(64), h * 192 + 128 + 32 + ar(32), h * 192 + 128 + ar(32)])]
    WUQ = _chunked(np.asarray(w_uq, f32)[0], cl, 4)
    cl = []
    for h in range(8):
        cl += [h * 256 + ar(128), h * 256 + 128 + ar(128)]
    WUKV = _chunked(np.asarray(w_ukv, f32)[0], cl, 2)
    WMKV = _chunked(np.asarray(w_mem_kv, f32)[0], [c * 128 + ar(128) for c in range(8)], 16)
    WPA = _chunked(np.asarray(w_p_moba, f32)[0], [c * 128 + ar(128) for c in range(16)], 8)
    WPB = _chunked(np.asarray(w_p_mla, f32)[0], [c * 128 + ar(128) for c in range(16)], 8)
    WPM = _chunked(np.asarray(w_p_mem, f32)[0], [c * 128 + ar(128) for c in range(16)], 4)
    WOUT = np.ascontiguousarray(np.asarray(w_out, f32)[0].reshape(16, 128, 2048).transpose(1, 0, 2))
    gcols = np.zeros((128, 40), f32)
    gcols[:, 0:16] = np.asarray(g_norm, f32)[0].reshape(16, 128).T
    gcols[:, 16:32] = np.asarray(g_mem, f32)[0].reshape(16, 128).T
    gcols[:, 32:36] = np.asarray(g_cq, f32)[0].reshape(4, 128).T
    gcols[:, 36:38] = np.asarray(g_ckv, f32)[0].reshape(2, 128).T
    gfin = np.asarray(g_final, f32).reshape(1, D)
    half = 32
    inv = (10000.0 ** (-np.arange(half, dtype=f32) / half)).astype(f32)
    i64 = np.arange(64)
    dd = np.arange(384) - 127
    E1 = np.zeros((33, 384), f32)
    bk = _t5_bucket(dd)
    for i in range(383):
        if dd[i] >= 0:
            E1[bk[i], i] += 1.0
            E1[31, i] -= 1.0
        else:
            E1[32, i] = 1.0
    rbaug = np.concatenate([np.asarray(rel_bias, f32), np.full((1, 8), NEG, f32)], axis=0)
    identb = np.eye(128, dtype=f32).astype(ml_dtypes.bfloat16)
    SELc = np.zeros((8, 8, 128), f32)
    for n in range(8):
        SELc[n, n, :] = 1.0
    SELc = SELc.reshape(8, 1024).astype(ml_dtypes.bfloat16)
    kk = np.arange(128)[:, None]
    qq = np.arange(128)[None, :]
    CAUSc = np.where(qq < kk, NEG, 0.0).astype(f32).astype(ml_dtypes.bfloat16)

    in_maps = []
    for c in range(NCORE):
        b, hf = c // 2, c % 2
        own = x[b, hf * 1024:(hf + 1) * 1024]
        ctx = x[b, 0:1024]
        xT = np.ascontiguousarray(np.concatenate([ctx, own], axis=0).T)
        pos = np.concatenate([np.arange(1024) + (hf - 1) * 1024, np.arange(1024) + hf * 1024]).astype(f32)
        ang = pos[None, :] * inv[i64 % 32][:, None]
        cosT = np.cos(ang).astype(f32)
        sn = np.sin(ang).astype(f32)
        ssinT = np.where((i64 < 32)[:, None], -sn, sn).astype(f32)
        pastb = np.zeros((128, 8, 8), f32)
        pasti = np.zeros((128, 8, 8), f32)
        npsel = np.zeros((128, 8, 8), f32)
        for i in range(8):
            ownblk = 4 + i // 2
            for n in range(8):
                past = (n < 4 and hf == 1) or (4 <= n < ownblk)
                pastb[:, i, n] = 0.0 if past else -1e30
                pasti[:, i, n] = 1.0 if past else 0.0
                npsel[:, i, n] = 0.0 if (past or n == ownblk) else NEG
        ctxb = np.full((128, 1), 0.0 if hf == 1 else NEG, f32)
        in_maps.append(dict(
            xT=xT, xo=np.ascontiguousarray(own), memT=np.ascontiguousarray(mem[b].T),
            WIN=WIN, WUQ=WUQ, WUKV=WUKV, WMKV=WMKV, WPA=WPA, WPB=WPB, WPM=WPM, WOUT=WOUT,
            gcols=gcols, gfin=gfin, cosT=cosT, ssinT=ssinT,
            pastb=pastb.reshape(128, 64), pasti=pasti.reshape(128, 64), npsel=npsel.reshape(128, 64),
            ctxb=ctxb, E1=E1, rbaug=rbaug, identb=identb, SELc=SELc, CAUSc=CAUSc))

    if "nc" not in _CACHE:
        _CACHE["nc"] = build_program()
    nc, _ = _CACHE["nc"]
    res = run_bass_kernel_spmd(nc, in_maps, core_ids=list(range(NCORE)))
    if DUMP:
        _CACHE["dump"] = [{k: np.asarray(res.results[c][k]) for k in ("dgA", "dgB", "dgM", "dyT", "dL", "dH")} for c in range(NCORE)]
    outp = np.zeros((B, S, D), f32)
    for c in range(NCORE):
        b, hf = c // 2, c % 2
        outp[b, hf * 1024:(hf + 1) * 1024] = res.results[c]["out"]
    return outp
```

```python
import contextlib
import math
import numpy as np
import ml_dtypes
import concourse.bass as bass
import concourse.mybir as mybir
from concourse.bass_utils import run_bass_kernel_spmd

F32 = mybir.dt.float32
BF16 = mybir.dt.bfloat16
AF = mybir.ActivationFunctionType
ALU = mybir.AluOpType
AX = mybir.AxisListType

D = 2048
S = 2048
B = 4
NCORE = 8
T_OWN = 1024
EPS = 1e-6
NEG = -30000.0
SC_MOBA = 128 ** -0.5
SC_MLA = 192 ** -0.5
SC_MEM = 128 ** -0.5

CH_MOBA = 0
CH_CKV = 32
CH_KR = 34
CH_CQ = 35
CH_ZB = 39
CH_QM = 47
CH_ZM = 51
CH_GL = 55
N_WIN = 103
STOP = 99
NH1 = 8
DBG = 99
LDBG = 99
DUMP = False


class Op:
    __slots__ = ("eng", "fn", "deps", "dsem", "signal", "count")

    def __init__(self, eng, fn, deps, dsem):
        self.eng = eng
        self.fn = fn
        self.deps = deps
        self.dsem = dsem
        self.signal = False
        self.count = 0


class Sched:
    ENGS = ("pe", "act", "dve", "pool", "sp")

    def __init__(self, nc, es):
        self.nc = nc
        self.es = es
        self.eng = dict(pe=nc.tensor, act=nc.scalar, dve=nc.vector, pool=nc.gpsimd, sp=nc.sync)
        self.ops = []
        self.lw = {}
        self.rd = {}
        self.last_on_eng = {}
        self.dma_pending = []

    def add(self, eng, fn, reads=(), writes=(), dsem=None):
        i = len(self.ops)
        deps = set()
        for k in reads:
            w = self.lw.get(k)
            if w is not None:
                deps.add(w)
            if k[0] == "ps":
                r = self.rd.get(k)
                if r:
                    deps.update(v for e_, v in r[0].items() if e_ != eng)
        for k in writes:
            w = self.lw.get(k)
            if w is not None:
                deps.add(w)
            r = self.rd.get(k)
            if r:
                deps.update(r[0].values())
                deps.update(r[1])
        for k in writes:
            self.lw[k] = i
            self.rd[k] = ({}, [])
        for k in reads:
            r = self.rd.get(k)
            if r is None:
                r = ({}, [])
                self.rd[k] = r
            if dsem is not None:
                r[1].append(i)
            else:
                r[0][eng] = i
        deps.discard(i)
        self.ops.append(Op(eng, fn, deps, dsem))
        self.last_on_eng[eng] = i
        if dsem is not None:
            self.dma_pending.append(i)
        return i

    def barrier(self):
        deps = set(self.last_on_eng.values()) | set(self.dma_pending)
        self.ops.append(Op("barrier", None, deps, None))
        self.dma_pending = []

    def emit(self):
        nc = self.nc
        ops = self.ops
        for op in ops:
            for d in op.deps:
                if op.eng == "pe" and ops[d].eng == "pe":
                    continue
                ops[d].signal = True
        cnt = {e: 0 for e in self.ENGS}
        dcnt = {}
        for op in ops:
            if op.eng == "barrier":
                continue
            if op.dsem is not None:
                dcnt[op.dsem] = dcnt.get(op.dsem, 0) + 16
                op.count = dcnt[op.dsem]
            elif op.signal:
                cnt[op.eng] += 1
                op.count = cnt[op.eng]
        sems = {}

        def sem(name):
            s = sems.get(name)
            if s is None:
                s = self.es.enter_context(nc.semaphore("s_" + name))
                sems[name] = s
            return s

        known = {e: {} for e in self.ENGS}

        def do_waits(e, deps):
            waits = {}
            for d in deps:
                dop = ops[d]
                if dop.dsem is not None:
                    nm = "d_" + dop.dsem
                else:
                    if dop.eng == "pe" and e == "pe":
                        continue
                    nm = "e_" + dop.eng
                if dop.count > waits.get(nm, 0):
                    waits[nm] = dop.count
            kn = known[e]
            for nm, val in waits.items():
                if kn.get(nm, 0) < val:
                    self.eng[e].wait_ge(sem(nm), val)
                    kn[nm] = val

        for op in ops:
            if op.eng == "barrier":
                for e in self.ENGS:
                    do_waits(e, op.deps)
                continue
            do_waits(op.eng, op.deps)
            ins = op.fn()
            if op.dsem is not None:
                ins.then_inc(sem("d_" + op.dsem), 16)
            elif op.signal:
                ins.then_inc(sem("e_" + op.eng), 1)
        self.stats = dict(n_ops=len(ops), cnt=cnt, n_sems=len(sems))


def build_program():
    nc = bass.Bass("TRN2", target_bir_lowering=False)

    def din(name, shape, dt=F32):
        return nc.dram_tensor(name, list(shape), dt, kind="ExternalInput").ap()

    xT = din("xT", [D, 2048])
    xo = din("xo", [T_OWN, D])
    memT = din("memT", [D, 256])
    WIN = din("WIN", [N_WIN, 128, 16, 128])
    WUQ = din("WUQ", [16, 128, 4, 128])
    WUKV = din("WUKV", [16, 128, 2, 128])
    WMKV = din("WMKV", [8, 128, 16, 128])
    WPA = din("WPA", [16, 128, 8, 128])
    WPB = din("WPB", [16, 128, 8, 128])
    WPM = din("WPM", [16, 128, 4, 128])
    WOUT = din("WOUT", [128, 16, 2048])
    gcols = din("gcols", [128, 40])
    gfin = din("gfin", [1, D])
    cosT = din("cosT", [64, 2048])
    ssinT = din("ssinT", [64, 2048])
    pastb = din("pastb", [128, 64])
    pasti = din("pasti", [128, 64])
    npsel = din("npsel", [128, 64])
    ctxb = din("ctxb", [128, 1])
    E1 = din("E1", [33, 384])
    rbaug = din("rbaug", [33, 8])
    identb = din("identb", [128, 128], BF16)
    SELc = din("SELc", [8, 8 * 128], BF16)
    CAUSc = din("CAUSc", [128, 128], BF16)
    out = nc.dram_tensor("out", [T_OWN, D], F32, kind="ExternalOutput").ap()
    if DUMP:
        dgA = nc.dram_tensor("dgA", [128, 8192], BF16, kind="ExternalOutput").ap()
        dgB = nc.dram_tensor("dgB", [128, 8192], BF16, kind="ExternalOutput").ap()
        dgM = nc.dram_tensor("dgM", [128, 4096], BF16, kind="ExternalOutput").ap()
        dyT = nc.dram_tensor("dyT", [128, 16384], BF16, kind="ExternalOutput").ap()
        dL = nc.dram_tensor("dL", [128, 10240], BF16, kind="ExternalOutput").ap()
        dH = nc.dram_tensor("dH", [128, 16384], BF16, kind="ExternalOutput").ap()
    scr = nc.dram_tensor("scr", [8, 128, 384], F32, kind="Internal")
    scr_ap = scr.ap()

    es = contextlib.ExitStack()
    with es:
        ARENA_KB = 200
        arena = es.enter_context(nc.sbuf_tensor("arena", [128, ARENA_KB * 512], BF16))
        banks = [es.enter_context(nc.psum_tensor(f"bank{i}", [128, 512], F32)) for i in range(8)]
        sch = Sched(nc, es)

        def vb(off_kb, nelem):
            o = int(off_kb * 512)
            return arena[:, o:o + nelem]

        def vf(off_kb, nelem):
            o = int(off_kb * 512)
            return arena[:, o:o + 2 * nelem].bitcast(F32)

        hTo = vb(0, 16 * 1024).rearrange("p (c t) -> p c t", c=16)
        hTc = vb(32, 16 * 1024).rearrange("p (c t) -> p c t", c=16)
        gA = vb(64, 8 * 1024).rearrange("p (c t) -> p c t", c=8)
        gB = vb(80, 8 * 1024).rearrange("p (c t) -> p c t", c=8)
        gM = vb(96, 4 * 1024).rearrange("p (c t) -> p c t", c=4)
        NSLOT = 4
        wsl = [vb(104 + 4 * i, 2048).rearrange("p (c n) -> p c n", c=16) for i in range(NSLOT)]
        ones_b = vb(120, 128)
        ident = vb(120.25, 128)
        caus = vb(120.5, 128)
        SEL = vb(120.75, 1024)
        Whi = vb(122.75, 8 * 256).rearrange("p (h x) -> p h x", h=8)
        Wlo = vb(126.75, 8 * 256).rearrange("p (h x) -> p h x", h=8)
        gc = vf(130.75, 40)
        ctxbias = vf(131, 1)
        zero_c = vf(131.25, 1)
        pastb_t = vf(131.5, 64)
        pasti_t = vf(131.75, 64)
        npsel_t = vf(132, 64)
        cos_own = vf(132.5, 1024)
        ssin_own = vf(136.5, 1024)
        TB = 140.5

        PSB = lambda i: ("ps", i)

        def mm(o, lhsT, rhs, start, stop, reads, writes):
            sch.add("pe", lambda: nc.tensor.matmul(o, lhsT=lhsT, rhs=rhs, start=start, stop=stop), reads, writes)

        def tr(o, in_, idn, reads, writes):
            sch.add("pe", lambda: nc.tensor.transpose(o, in_, idn), reads, writes)

        def act(o, in_, func, reads, writes, scale=1.0, bias=0.0, accum_out=None):
            if accum_out is None:
                sch.add("act", lambda: nc.scalar.activation(out=o, in_=in_, func=func, bias=bias, scale=scale), reads, writes)
            else:
                sch.add("act", lambda: nc.scalar.activation(out=o, in_=in_, func=func, bias=bias, scale=scale, accum_out=accum_out), reads, writes)

        def tt(o, a, b, op, reads, writes, eng="dve"):
            e = nc.vector if eng == "dve" else nc.gpsimd
            sch.add(eng, lambda: e.tensor_tensor(out=o, in0=a, in1=b, op=op), reads, writes)

        def ts(o, a, s1, s2, op0, op1, reads, writes):
            if op1 is None:
                sch.add("dve", lambda: nc.vector.tensor_scalar(out=o, in0=a, scalar1=s1, scalar2=None, op0=op0), reads, writes)
            else:
                sch.add("dve", lambda: nc.vector.tensor_scalar(out=o, in0=a, scalar1=s1, scalar2=s2, op0=op0, op1=op1), reads, writes)

        def stt(o, a, s, b, op0, op1, reads, writes):
            sch.add("dve", lambda: nc.vector.scalar_tensor_tensor(out=o, in0=a, scalar=s, in1=b, op0=op0, op1=op1), reads, writes)

        def cp(o, a, reads, writes):
            sch.add("dve", lambda: nc.vector.tensor_copy(out=o, in_=a), reads, writes)

        def recip(o, a, reads, writes):
            sch.add("dve", lambda: nc.vector.reciprocal(out=o, in_=a), reads, writes)

        def memset(o, val, writes):
            sch.add("dve", lambda: nc.vector.memset(o, val), (), writes)

        def dma(q, o, in_, dsem, reads, writes):
            e = {"sp": nc.sync, "pool": nc.gpsimd}[q]
            sch.add(q, lambda: e.dma_start(out=o, in_=in_), reads, writes, dsem=dsem)

        memset(ones_b, 1.0, [("ones",)])
        memset(zero_c, 0.0, [("zero",)])
        dma("sp", ident, identb, "c_ident", (), [("ident",)])
        dma("sp", caus, CAUSc, "c_caus", (), [("caus",)])
        memset(SEL, 0.0, [("SEL",)])
        dma("sp", SEL[0:8, :], SELc, "c_sel", (), [("SEL",)])
        dma("sp", gc, gcols, "c_gc", (), [("gc",)])
        dma("sp", ctxbias, ctxb, "c_ctxb", (), [("ctxb",)])
        dma("sp", pastb_t, pastb, "c_pb", (), [("pastb",)])
        dma("sp", pasti_t, pasti, "c_pi", (), [("pasti",)])
        dma("sp", npsel_t, npsel, "c_np", (), [("npsel",)])
        dma("sp", cos_own[0:64, :], cosT[:, 1024:2048], "c_cos", (), [("cos_own",)])
        dma("sp", ssin_own[0:64, :], ssinT[:, 1024:2048], "c_sin", (), [("ssin_own",)])

        ts(cos_own[0:64, :], cos_own[0:64, :], SC_MLA, None, ALU.mult, None, [("cos_own",)], [("cos_own",)])
        ts(ssin_own[0:64, :], ssin_own[0:64, :], SC_MLA, None, ALU.mult, None, [("ssin_own",)], [("ssin_own",)])

        wstate = {"n": 0}

        def wload(src_ap, kc):
            s = wstate["n"] % NSLOT
            wstate["n"] += 1
            key = ("w", s)
            dma("pool", wsl[s][:, 0:kc, :], src_ap, f"w{s}", (), [key])
            return wsl[s], key

        class WStream:
            def __init__(self, specs, look=3):
                self.specs = specs
                self.look = look
                self.loaded = []

            def get(self, i):
                while len(self.loaded) < min(len(self.specs), i + self.look):
                    src, kc = self.specs[len(self.loaded)]
                    self.loaded.append(wload(src, kc))
                return self.loaded[i]

        _sp = []
        for h in range(8):
            for r in range(4):
                _sp.append((WIN[CH_MOBA + 4 * h + r], 16))
        ws_B1 = WStream(_sp)
        ws_B2 = WStream([(WIN[CH_CKV], 16), (WIN[CH_CKV + 1], 16), (WIN[CH_KR], 16)] + [(WIN[CH_CQ + c], 16) for c in range(4)], look=3)
        _sp = []
        for h in range(8):
            _sp += [(WUKV[2 * h], 2), (WUKV[2 * h + 1], 2), (WUQ[2 * h], 4), (WUQ[2 * h + 1], 4), (WIN[CH_ZB + h], 16)]
        ws_B3 = WStream(_sp, look=4)
        _sp = []
        for hm in range(4):
            _sp += [(WMKV[hm], 16), (WMKV[4 + hm], 16), (WIN[CH_QM + hm], 16), (WIN[CH_ZM + hm], 16)]
        ws_B4 = WStream(_sp)
        _gw = [(WPA, 8), (WPB, 8), (WPM, 4)]
        _sp = []
        for c in range(16):
            for br in range(3):
                _sp += [(WIN[CH_GL + 16 * br + c], 16), (_gw[br][0][c], _gw[br][1])]
        ws_D = WStream(_sp)

        pbank = {"n": 0, "list": [4, 5, 6, 7]}

        def next_pbank():
            l = pbank["list"]
            b = l[pbank["n"] % len(l)]
            pbank["n"] += 1
            return b

        def proj(wv, wkey, kc_n, rhs_fn, evac, m_lo=0, m_hi=128, ncols=512):
            b = next_pbank()
            M = m_hi - m_lo
            o = banks[b][0:M, 0:ncols]
            for kc in range(kc_n):
                r_ap, r_key = rhs_fn(kc)
                mm(o, wv[:, kc, m_lo:m_hi], r_ap, kc == 0, kc == kc_n - 1, [wkey, r_key], [PSB(b)])
            evac(o, PSB(b))

        def hT(kc, tg):
            if tg < 2:
                return hTc[:, kc, tg * 512:(tg + 1) * 512], ("hTc", kc, tg)
            return hTo[:, kc, (tg - 2) * 512:(tg - 1) * 512], ("hTo", kc, tg - 2)

        def rms_stage(src, ntg, ncols, gcol0, dst_fn, TBo, alt_off=None):
            xs_sets = [[vf(TBo + 2 * c, 512) for c in range(16)]]
            xs_offs = [TBo]
            if alt_off is not None:
                xs_sets.append([vf(alt_off + 2 * c, 512) for c in range(16)])
                xs_offs.append(alt_off)
            rstd = vf(TBo + 32, 512)
            sqs = [vb(TBo + 34 + i, 512) for i in range(3)]
            tmp = vf(TBo + 37, 512)
            for tg in range(ntg):
                b = next_pbank()
                par = tg % len(xs_sets)
                xs = xs_sets[par]
                if False and ncols == 512:
                    base = xs_offs[par]
                    for cg in range(4):
                        dst4 = vf(base + 8 * cg, 4 * 512).rearrange("p (c t) -> p c t", c=4)
                        src4 = src[cg * 512:(cg + 1) * 512, tg * ncols:(tg + 1) * ncols].rearrange("(c p) t -> p c t", p=128)
                        dma("sp", dst4, src4, f"xs{par}_{cg}", (), [("xs", par, 4 * cg + i) for i in range(4)])
                for c in range(16):
                    if True:
                        dma("sp", xs[c][:, 0:ncols], src[c * 128:(c + 1) * 128, tg * ncols:(tg + 1) * ncols],
                            f"xs{par}_{c}", (), [("xs", par, c)])
                    sq = sqs[c % 3]
                    if DBG < 2:
                        continue
                    act(sq[:, 0:ncols], xs[c][:, 0:ncols], AF.Square, [("xs", par, c)], [("sq", c % 3)])
                    if DBG < 3:
                        continue
                    mm(banks[b][:, 0:ncols], ones_b, sq[:, 0:ncols], c == 0, c == 15,
                       [("ones",), ("sq", c % 3)], [PSB(b)])
                if DBG < 4:
                    continue
                ts(tmp[:, 0:ncols], banks[b][:, 0:ncols], 1.0 / D, EPS, ALU.mult, ALU.add, [PSB(b)], [("rtmp",)])
                act(tmp[:, 0:ncols], tmp[:, 0:ncols], AF.Sqrt, [("rtmp",)], [("rtmp",)])
                recip(rstd[:, 0:ncols], tmp[:, 0:ncols], [("rtmp",)], [("rstd",)])
                if DBG < 5:
                    continue
                for c in range(16):
                    d_ap, d_key = dst_fn(c, tg)
                    stt(d_ap, xs[c][:, 0:ncols], gc[:, gcol0 + c:gcol0 + c + 1], rstd[:, 0:ncols],
                        ALU.mult, ALU.mult, [("xs", par, c), ("gc",), ("rstd",)], [d_key])

        vreps = [vf(180.5 + 1.5 * h, 384) for h in range(8)]
        e1_t = vf(192.5, 384)
        rb_t = vf(194, 8)
        rbrep = vf(194.25, 128)
        ones_f = vf(194.75, 128)
        wd = vf(196, 256)
        wfs = [vf(96 + h, 256) for h in range(8)]
        dma("sp", e1_t[0:33, :], E1, "c_e1", (), [("e1",)])
        dma("sp", rb_t[0:33, :], rbaug, "c_rb", (), [("rb",)])
        memset(ones_f[0:33, :], 1.0, [("ones_f",)])
        for h in range(8 if STOP >= 2 else 0):
            ts(rbrep[0:33, :], ones_f[0:33, :], rb_t[0:33, h:h + 1], None, ALU.mult, None,
               [("ones_f",), ("rb",)], [("rbrep",)])
            b = next_pbank()
            mm(banks[b][:, 0:384], rbrep[0:33, :], e1_t[0:33, :], True, True, [("rbrep",), ("e1",)], [PSB(b)])
            cp(vreps[h], banks[b][:, 0:384], [PSB(b)], [("vrep", h)])
            dma("sp", scr_ap[h], vreps[h], f"scrw{h}", [("vrep", h)], [("scr", h)])
            skew = bass.AP(scr, h * 128 * 384 + 127, [[383, 128], [1, 256]])
            dma("sp", wfs[h], skew, f"scrr{h}", [("scr", h)], [("wf", h)])
        toep_tail = []
        for h in range(8 if STOP >= 2 else 0):
            toep_tail.append(h)

        if STOP >= 1:
            rms_stage(xT, 4, 512, 0, hT, TB, alt_off=64)
        for h in toep_tail:
            cp(Whi[:, h, :], wfs[h], [("wf", h)], [("Whi", h)])
            tt(wd, wfs[h], Whi[:, h, :], ALU.subtract, [("wf", h), ("Whi", h)], [("wd",)])
            cp(Wlo[:, h, :], wd, [("wd",)], [("Wlo", h)])


        ws_B1.get(0)
        sch.barrier()

        PT = [vb(TB + i, 512) for i in range(4)]
        rden = [vf(TB + 4, 512), vf(TB + 190.5 - 140.5, 512)]
        otmp = [vf(TB + 6, 512), vf(TB + 192.5 - 140.5, 512)]
        att = {"n": 0, "pair": 0, "S": [0, 1], "LA": 1, "pairs": [(2, 3), (4, 5)]}

        def attention(n_ktiles_fn, logit_mms, exp_bias_fn, v_fn, dst_fn, siluz_fn, col0_fn):
            LA = att["LA"]
            Sb = att["S"]
            for G in range(2):
                ob, db = att["pairs"][att["pair"] % 2]
                att["pair"] += 1
                nk = n_ktiles_fn(G)
                info = {}
                for jj in range(nk + LA):
                    if jj < nk:
                        j = jj
                        c0 = col0_fn(G, j)
                        N = 512 - c0
                        sb = Sb[att["n"] % len(Sb)]
                        slot = att["n"] % 4
                        att["n"] += 1
                        logit_mms(G, j, c0, banks[sb][:, 0:N], PSB(sb))
                        bias, bkey = exp_bias_fn(j)
                        act(PT[slot][:, 0:N], banks[sb][:, 0:N], AF.Exp, [PSB(sb)] + ([bkey] if bkey else []),
                            [("PT", slot)], bias=bias)
                        info[j] = (c0, N, slot)
                    if jj >= LA:
                        j = jj - LA
                        c0, N, slot = info[j]
                        v_ap, v_key = v_fn(j)
                        mm(banks[ob][:, c0:512], v_ap, PT[slot][:, 0:N], j == 0, j == nk - 1, [v_key, ("PT", slot)], [PSB(ob)])
                        mm(banks[db][:, c0:512], ones_b, PT[slot][:, 0:N], j == 0, j == nk - 1, [("ones",), ("PT", slot)], [PSB(db)])
                    yield
                pr = att["pair"] % 2
                recip(rden[pr], banks[db][:, :], [PSB(db)], [("rden", pr)])
                tt(otmp[pr], banks[ob][:, :], rden[pr], ALU.mult, [PSB(ob), ("rden", pr)], [("otmp", pr)])
                d_ap, d_key = dst_fn(G)
                z_ap, z_key = siluz_fn(G)
                tt(d_ap, otmp[pr], z_ap, ALU.mult, [("otmp", pr), z_key], [d_key])

        def interleave(ga, gp, na=2, npj=1):
            a_done = ga is None
            p_done = gp is None
            while not (a_done and p_done):
                for _ in range(na):
                    if not a_done:
                        try:
                            next(ga)
                        except StopIteration:
                            a_done = True
                for _ in range(npj):
                    if not p_done:
                        try:
                            next(gp)
                        except StopIteration:
                            p_done = True

        def transpose_v(vT, vT_key, ntiles, dst_fn):
            for j4 in range(0, ntiles, 4):
                n = min(4, ntiles - j4)
                b = next_pbank()
                pb = banks[b][:, :].bitcast(BF16)
                for jj in range(n):
                    tr(pb[:, jj * 128:(jj + 1) * 128], vT[:, (j4 + jj) * 128:(j4 + jj + 1) * 128], ident,
                       [vT_key, ("ident",)], [PSB(b)])
                d_ap, d_key = dst_fn(j4, n)
                cp(d_ap, pb[:, 0:n * 128], [PSB(b)], [d_key])

        MB = TB + 8
        kT = [vb(MB + 4 * i, 2048) for i in range(2)]
        vtok = [vb(MB + 8 + 4 * i, 2048) for i in range(2)]
        qT = [vb(MB + 16 + 2 * i, 1024) for i in range(2)]
        szT = [vb(MB + 20 + 2 * i, 1024) for i in range(2)]
        vT_tmp = vb(MB + 24, 2048)
        selbT = [vb(MB + 28 + 2 * i, 1024) for i in range(2)]
        ksum = vf(MB + 32, 8)
        kmean_b = vb(MB + 32.25, 8)
        gm = vf(MB + 32.5, 64)
        mx8 = vf(MB + 33, 8)
        selb1 = vf(MB + 33.25, 64)
        selb2 = vf(MB + 33.75, 64)
        selb3 = vb(MB + 34.25, 64)

        pbank["list"] = [6, 7]
        for i in range(2):
            memset(selbT[i], 0.0, [("selbT", i)])
        specs = []
        for h in range(8):
            for r in range(4):
                specs.append((WIN[CH_MOBA + 4 * h + r], 16))
        ws = ws_B1
        def b1_proj(h):
            p = h % 2
            kT_h, vt_h, qT_h, sz_h, sb_h = kT[p], vtok[p], qT[p], szT[p], selbT[p]
            wv, wk = ws.get(4 * h + 0)
            for tg in range(4):
                def ev_k(o, ok, tg=tg):
                    for hh in range(2):
                        act(kT_h[:, tg * 512 + hh * 256: tg * 512 + (hh + 1) * 256], o[:, hh * 256:(hh + 1) * 256], AF.Copy,
                            [ok], [("kT", p, tg), ("ksum", tg)], accum_out=ksum[:, 2 * tg + hh:2 * tg + hh + 1])
                proj(wv, wk, 16, lambda kc, tg=tg: hT(kc, tg), ev_k)
                yield
            ts(kmean_b, ksum, 1.0 / 256, None, ALU.mult, None, [("ksum", t) for t in range(4)], [("kmean",)])
            wv, wk = ws.get(4 * h + 1)
            for tg in range(4):
                def ev_v(o, ok, tg=tg):
                    cp(vT_tmp[:, tg * 512:(tg + 1) * 512], o, [ok], [("vT", tg)])
                proj(wv, wk, 16, lambda kc, tg=tg: hT(kc, tg), ev_v)
                yield
            for tg in range(4):
                transpose_v(vT_tmp[:, tg * 512:(tg + 1) * 512], ("vT", tg), 4,
                            lambda j4, n, tg=tg: (vt_h[:, (tg * 4) * 128:(tg * 4 + 4) * 128], ("vtok", p, tg)))
                yield
            wv, wk = ws.get(4 * h + 2)
            for tg in range(2):
                def ev_q(o, ok, tg=tg):
                    act(qT_h[:, tg * 512:(tg + 1) * 512], o, AF.Copy, [ok], [("qT", p, tg)], scale=SC_MOBA)
                proj(wv, wk, 16, lambda kc, tg=tg: hT(kc, tg + 2), ev_q)
                yield
            wv, wk = ws.get(4 * h + 3)
            for tg in range(2):
                def ev_z(o, ok, tg=tg):
                    act(sz_h[:, tg * 512:(tg + 1) * 512], o, AF.Silu, [ok], [("szT", p, tg)])
                proj(wv, wk, 16, lambda kc, tg=tg: hT(kc, tg + 2), ev_z)
                yield
            b = next_pbank()
            for i in range(8):
                mm(banks[b][:, i * 8:(i + 1) * 8], qT_h[:, i * 128:(i + 1) * 128], kmean_b, True, True,
                   [("qT", p, i // 4), ("kmean",)], [PSB(b)])
            tt(gm, banks[b][:, 0:64], pastb_t, ALU.add, [PSB(b), ("pastb",)], [("gm",)])
            for i in range(8):
                sl = slice(i * 8, (i + 1) * 8)
                sch.add("dve", lambda sl=sl: nc.vector.max(out=mx8, in_=gm[:, sl]), [("gm",)], [("mx8",)])
                ts(selb1[:, sl], gm[:, sl], mx8[:, 2:3], NEG, ALU.is_lt, ALU.mult, [("gm",), ("mx8",)], [("selb1", i)])
            tt(selb2, selb1, pasti_t, ALU.mult, [("selb1", i) for i in range(8)] + [("pasti",)], [("selb2",)])
            tt(selb3, selb2, npsel_t, ALU.add, [("selb2",), ("npsel",)], [("selb3",)])
            yield
            b = next_pbank()
            pb = banks[b][:, :].bitcast(BF16)
            for i in range(8):
                tr(pb[0:8, i * 128:(i + 1) * 128], selb3[:, i * 8:(i + 1) * 8], ident, [("selb3",), ("ident",)], [PSB(b)])
            cp(sb_h[0:8, :], pb[0:8, 0:1024], [PSB(b)], [("selbT", p)])
            yield

        def b1_attn(h):
            p = h % 2
            kT_h, vt_h, qT_h, sz_h, sb_h = kT[p], vtok[p], qT[p], szT[p], selbT[p]

            def lm(G, j, c0, o, ok):
                N = 512 - c0
                q0 = 512 * G + c0
                n = j // 2
                ip = j - 8
                extra = []
                for (qt, xo_) in ((ip, 0), (ip + 1, 128)):
                    if qt < 0 or qt > 7:
                        continue
                    cc = qt * 128 - 512 * G
                    if cc < c0 or cc >= 512:
                        continue
                    extra.append((cc - c0, xo_))
                mm(o, kT_h[:, j * 128:(j + 1) * 128], qT_h[:, q0:q0 + N], True, False,
                   [("kT", p, j // 4), ("qT", p, G)], [ok])
                for (cc, xo_) in extra:
                    mm(o[:, cc:cc + 128], ident, Whi[:, h, xo_:xo_ + 128], False, False, [("ident",), ("Whi", h)], [ok])
                    mm(o[:, cc:cc + 128], ident, Wlo[:, h, xo_:xo_ + 128], False, False, [("ident",), ("Wlo", h)], [ok])
                mm(o, SEL[:, n * 128:(n + 1) * 128], sb_h[:, q0:q0 + N], False, True,
                   [("SEL",), ("selbT", p)], [ok])

            yield from attention(lambda G: 8 + 4 * G + 4, lm, lambda j: (0.0, None),
                                 lambda j: (vt_h[:, j * 128:(j + 1) * 128], ("vtok", p, j // 4)),
                                 lambda G: (gA[:, h, G * 512:(G + 1) * 512], ("gA", h, G)),
                                 lambda G: (sz_h[:, G * 512:(G + 1) * 512], ("szT", p, G)),
                                 lambda G, j: max(0, j - 8 - 4 * G) * 128)

        nh1 = NH1 if STOP >= 3 else 0
        if nh1 > 0:
            for _ in b1_proj(0):
                pass
        for h in range(nh1):
            interleave(b1_attn(h), b1_proj(h + 1) if h + 1 < nh1 else None, 3, 2)

        ws_B2.get(0)
        sch.barrier()

        pbank["list"] = [4, 5, 6, 7]
        LB = TB + 8
        ckvn = vb(LB, 2 * 2048).rearrange("p (c t) -> p c t", c=2)
        cqn = vb(LB + 8, 4 * 1024).rearrange("p (c t) -> p c t", c=4)
        kropeT = vb(LB + 16, 2048)
        sq2 = [vb(LB + 20 + i, 512) for i in range(2)]
        rtmp2 = vf(LB + 22, 512)
        rstd2 = vf(LB + 24, 512)
        cs_t = vf(LB + 26, 512)
        sn_t = vf(LB + 28, 512)
        t1 = vf(LB + 30, 512)
        t2 = vf(LB + 32, 512)

        memset(kropeT, 0.0, [("kropeT", t) for t in range(4)])
        specs = [(WIN[CH_CKV], 16), (WIN[CH_CKV + 1], 16), (WIN[CH_KR], 16)] + [(WIN[CH_CQ + c], 16) for c in range(4)]
        ws = ws_B2

        def latent(nch, ws_i0, tgs, tg_off, gcol0, dst, dst_name, dim):
            for tg in tgs:
                sb_ = 1
                for c in range(nch):
                    wv, wk = ws.get(ws_i0 + c)

                    def ev(o, ok, c=c, tg=tg):
                        ts(dst[:, c, tg * 512:(tg + 1) * 512], o, gc[:, gcol0 + c:gcol0 + c + 1], None, ALU.mult, None,
                           [ok, ("gc",)], [(dst_name, c, tg)])
                        if LDBG < 1:
                            return
                        act(sq2[c % 2], o, AF.Square, [ok], [("sq2", c % 2)])
                        if LDBG < 2:
                            return
                        mm(banks[sb_][:, :], ones_b, sq2[c % 2], c == 0, c == nch - 1, [("ones",), ("sq2", c % 2)], [PSB(sb_)])
                    proj(wv, wk, 16, lambda kc, tg=tg: hT(kc, tg + tg_off), ev)
                if LDBG < 3:
                    continue
                ts(rtmp2, banks[sb_][:, :], 1.0 / dim, EPS, ALU.mult, ALU.add, [PSB(sb_)], [("rtmp2",)])
                act(rtmp2, rtmp2, AF.Sqrt, [("rtmp2",)], [("rtmp2",)])
                recip(rstd2, rtmp2, [("rtmp2",)], [("rstd2",)])
                if LDBG < 4:
                    continue
                for c in range(nch):
                    tt(dst[:, c, tg * 512:(tg + 1) * 512], dst[:, c, tg * 512:(tg + 1) * 512], rstd2, ALU.mult,
                       [(dst_name, c, tg), ("rstd2",)], [(dst_name, c, tg)])

        if STOP < 4:
            sch.barrier()
            sch.emit()
            return nc, sch
        latent(2, 0, range(4), 0, 36, ckvn, "ckvn", 256)

        def rope_pair(wv, wk, kc_n, rhs_fn, cos_ap, cos_key, sin_ap, sin_key, dst_ap, dst_key, scale):
            hold = {}

            def evA(o, ok):
                tt(t1[0:64, :], o, cos_ap, ALU.mult, [ok, cos_key], [("t1",)])

            def evB(o, ok):
                tt(t2[0:64, :], o, sin_ap, ALU.mult, [ok, sin_key], [("t2",)])
            proj(wv, wk, kc_n, rhs_fn, evA, 0, 64)
            proj(wv, wk, kc_n, rhs_fn, evB, 64, 128)
            stt(dst_ap, t1[0:64, :], scale, t2[0:64, :], ALU.mult, ALU.add, [("t1",), ("t2",)], [dst_key])

        wv, wk = ws.get(2)
        for tg in range(4 if DBG >= 21 else 0):
            dma("sp", cs_t[0:64, :], cosT[:, tg * 512:(tg + 1) * 512], "cs_t", (), [("cs_t",)])
            dma("sp", sn_t[0:64, :], ssinT[:, tg * 512:(tg + 1) * 512], "sn_t", (), [("sn_t",)])
            rope_pair(wv, wk, 16, lambda kc, tg=tg: hT(kc, tg), cs_t[0:64, :], ("cs_t",), sn_t[0:64, :], ("sn_t",),
                      kropeT[0:64, tg * 512:(tg + 1) * 512], ("kropeT", tg), 1.0)

        if DBG >= 22:
            latent(4, 3, range(2), 2, 32, cqn, "cqn", 512)

        ws_B3.get(0)
        sch.barrier()

        pbank["list"] = [6, 7]
        att.update(S=[0, 1], LA=1, pairs=[(2, 3), (4, 5)])
        HB = 32
        knT = [vb(HB + 4 * i, 2048) for i in range(2)]
        vtb = [vb(HB + 8 + 4 * i, 2048) for i in range(2)]
        qnT = [vb(HB + 16 + 2 * i, 1024) for i in range(2)]
        qrT = [vb(HB + 20 + 2 * i, 1024) for i in range(2)]
        szB = [vb(HB + 24 + 2 * i, 1024) for i in range(2)]
        vT2 = vb(HB + 28, 2048)

        for i in range(2):
            memset(qrT[i], 0.0, [("qrT", i, 0), ("qrT", i, 1)])
        specs = []
        for h in range(8):
            specs += [(WUKV[2 * h], 2), (WUKV[2 * h + 1], 2), (WUQ[2 * h], 4), (WUQ[2 * h + 1], 4), (WIN[CH_ZB + h], 16)]
        ws = ws_B3
        if STOP < 5:
            sch.barrier()
            sch.emit()
            return nc, sch
        def b3_proj(h):
            p = h % 2
            kn_h, vt_h, qn_h, qr_h, sz_h = knT[p], vtb[p], qnT[p], qrT[p], szB[p]
            wv, wk = ws.get(5 * h + 0)
            for tg in range(4):
                def ev_k(o, ok, tg=tg):
                    act(kn_h[:, tg * 512:(tg + 1) * 512], o, AF.Copy, [ok], [("knT", p, tg)])
                proj(wv, wk, 2, lambda kc, tg=tg: (ckvn[:, kc, tg * 512:(tg + 1) * 512], ("ckvn", kc, tg)), ev_k)
                yield
            wv, wk = ws.get(5 * h + 1)
            for tg in range(4):
                def ev_v(o, ok, tg=tg):
                    cp(vT2[:, tg * 512:(tg + 1) * 512], o, [ok], [("vT2", tg)])
                proj(wv, wk, 2, lambda kc, tg=tg: (ckvn[:, kc, tg * 512:(tg + 1) * 512], ("ckvn", kc, tg)), ev_v)
                yield
            for tg in range(4):
                transpose_v(vT2[:, tg * 512:(tg + 1) * 512], ("vT2", tg), 4,
                            lambda j4, n, tg=tg: (vt_h[:, (tg * 4) * 128:(tg * 4 + 4) * 128], ("vtb", p, tg)))
                yield
            wv, wk = ws.get(5 * h + 2)
            for tg in range(2):
                def ev_q(o, ok, tg=tg):
                    act(qn_h[:, tg * 512:(tg + 1) * 512], o, AF.Copy, [ok], [("qnT", p, tg)], scale=SC_MLA)
                proj(wv, wk, 4, lambda kc, tg=tg: (cqn[:, kc, tg * 512:(tg + 1) * 512], ("cqn", kc, tg)), ev_q)
                yield
            wv, wk = ws.get(5 * h + 3)
            for tg in range(2):
                rope_pair(wv, wk, 4, lambda kc, tg=tg: (cqn[:, kc, tg * 512:(tg + 1) * 512], ("cqn", kc, tg)),
                          cos_own[0:64, tg * 512:(tg + 1) * 512], ("cos_own",),
                          ssin_own[0:64, tg * 512:(tg + 1) * 512], ("ssin_own",),
                          qr_h[0:64, tg * 512:(tg + 1) * 512], ("qrT", p, tg), 1.0)
                yield
            wv, wk = ws.get(5 * h + 4)
            for tg in range(2):
                def ev_z(o, ok, tg=tg):
                    act(sz_h[:, tg * 512:(tg + 1) * 512], o, AF.Silu, [ok], [("szB", p, tg)])
                proj(wv, wk, 16, lambda kc, tg=tg: hT(kc, tg + 2), ev_z)
                yield

        def b3_attn(h):
            p = h % 2
            kn_h, vt_h, qn_h, qr_h, sz_h = knT[p], vtb[p], qnT[p], qrT[p], szB[p]

            def lm(G, j, c0, o, ok):
                N = 512 - c0
                q0 = 512 * G + c0
                diag = (j >= 8 and 0 <= (j - 8) * 128 - 512 * G < 512)
                mm(o, kn_h[:, j * 128:(j + 1) * 128], qn_h[:, q0:q0 + N], True, False,
                   [("knT", p, j // 4), ("qnT", p, G)], [ok])
                if diag:
                    cc = (j - 8) * 128 - 512 * G - c0
                    mm(o[:, cc:cc + 128], ident, caus, False, False, [("ident",), ("caus",)], [ok])
                mm(o, kropeT[:, j * 128:(j + 1) * 128], qr_h[:, q0:q0 + N], False, True,
                   [("kropeT", j // 4), ("qrT", p, G)], [ok])

            yield from attention(lambda G: 8 + 4 * G + 4, lm,
                                 lambda j: ((ctxbias, ("ctxb",)) if j < 8 else (0.0, None)),
                                 lambda j: (vt_h[:, j * 128:(j + 1) * 128], ("vtb", p, j // 4)),
                                 lambda G: (gB[:, h, G * 512:(G + 1) * 512], ("gB", h, G)),
                                 lambda G: (sz_h[:, G * 512:(G + 1) * 512], ("szB", p, G)),
                                 lambda G, j: max(0, j - 8 - 4 * G) * 128)

        for _ in b3_proj(0):
            pass
        for h in range(8):
            interleave(b3_attn(h), b3_proj(h + 1) if h + 1 < 8 else None, 3, 2)

        ws_B4.get(0)
        sch.barrier()
        if DUMP:
            dma("sp", dL, vb(LB, 10240), "dump4", (), [("dump", 4)])
            dma("sp", dH, vb(HB, 16384), "dump5", (), [("dump", 5)])
            sch.barrier()

        mT = vb(HB, 16 * 256).rearrange("p (c t) -> p c t", c=16)
        kmT = vb(HB + 8, 4 * 256).rearrange("p (c t) -> p c t", c=4)
        vmtok = vb(HB + 10, 2 * 512)
        qmT = vb(HB + 12, 1024)
        szM = vb(HB + 14, 1024)
        vmT = vb(HB + 16, 256)

        if STOP < 6:
            sch.barrier()
            sch.emit()
            return nc, sch

        def mT_dst(c, tg):
            return mT[:, c, :], ("mT", c)
        rms_stage(memT, 1, 256, 16, mT_dst, TB + 8)

        specs = []
        for hm in range(4):
            specs += [(WMKV[hm], 16), (WMKV[4 + hm], 16), (WIN[CH_QM + hm], 16), (WIN[CH_ZM + hm], 16)]
        ws = ws_B4
        for hm in range(4):
            wv, wk = ws.get(4 * hm + 0)

            def ev_k(o, ok, hm=hm):
                act(kmT[:, hm, :], o, AF.Copy, [ok], [("kmT", hm)])
            proj(wv, wk, 16, lambda kc: (mT[:, kc, :], ("mT", kc)), ev_k, ncols=256)
            wv, wk = ws.get(4 * hm + 1)

            def ev_v(o, ok):
                cp(vmT, o, [ok], [("vmT",)])
            proj(wv, wk, 16, lambda kc: (mT[:, kc, :], ("mT", kc)), ev_v, ncols=256)
            b = next_pbank()
            pb = banks[b][:, :].bitcast(BF16)
            for jj in range(2):
                tr(pb[:, jj * 128:(jj + 1) * 128], vmT[:, jj * 128:(jj + 1) * 128], ident, [("vmT",), ("ident",)], [PSB(b)])
            for jj in range(2):
                cp(vmtok[:, jj * 512 + hm * 128: jj * 512 + (hm + 1) * 128], pb[:, jj * 128:(jj + 1) * 128],
                   [PSB(b)], [("vmtok", jj, hm)])
            wv, wk = ws.get(4 * hm + 2)
            for tg in range(2):
                def ev_q(o, ok, tg=tg):
                    act(qmT[:, tg * 512:(tg + 1) * 512], o, AF.Copy, [ok], [("qmT", tg)], scale=SC_MEM)
                proj(wv, wk, 16, lambda kc, tg=tg: hT(kc, tg + 2), ev_q)
            wv, wk = ws.get(4 * hm + 3)
            for tg in range(2):
                def ev_z(o, ok, tg=tg):
                    act(szM[:, tg * 512:(tg + 1) * 512], o, AF.Silu, [ok], [("szM", tg)])
                proj(wv, wk, 16, lambda kc, tg=tg: hT(kc, tg + 2), ev_z)

            def lm(G, j, c0, o, ok, hm=hm):
                mm(o, kmT[:, hm, j * 128:(j + 1) * 128], qmT[:, 512 * G:512 * G + 512], True, True,
                   [("kmT", hm), ("qmT", G)], [ok])

            for _ in attention(lambda G: 2, lm, lambda j: (0.0, None),
                      lambda j, hm=hm: (vmtok[:, j * 512 + hm * 128: j * 512 + (hm + 1) * 128], ("vmtok", j, hm)),
                      lambda G, hm=hm: (gM[:, hm, G * 512:(G + 1) * 512], ("gM", hm, G)),
                      lambda G: (szM[:, G * 512:(G + 1) * 512], ("szM", G)),
                      lambda G, j: 0):
                pass

        ws_D.get(0)
        sch.barrier()

        if STOP < 7:
            sch.barrier()
            sch.emit()
            return nc, sch
        pbank["list"] = [0, 1, 2, 3, 4, 5, 6, 7]
        yT = vb(TB + 8, 16 * 1024).rearrange("p (c t) -> p c t", c=16)
        sig = [vf(32 + 2 * i, 512) for i in range(3)]
        yacc0 = [vf(38 + 2 * i, 512) for i in range(2)]
        yacc1 = [vf(42 + 2 * i, 512) for i in range(2)]
        ytmp = [vf(TB + 2 * i, 512) for i in range(2)]
        wo_slots = [vb(46, 16 * 512).rearrange("p (c n) -> p c n", c=16), vb(62, 16 * 512).rearrange("p (c n) -> p c n", c=16)]
        gsrc = [(gA, "gA", 8, WPA), (gB, "gB", 8, WPB), (gM, "gM", 4, WPM)]
        specs = []
        for c in range(16):
            for br in range(3):
                specs += [(WIN[CH_GL + 16 * br + c], 16), (gsrc[br][3][c], gsrc[br][2])]
        ws = ws_D
        for c in range(16):
            if c == 12:
                wo_load(dma, wo_slots[0], WOUT, 0)
            for br in range(3):
                gten, gname, gk, _ = gsrc[br]
                wvg, wkg = ws.get((c * 3 + br) * 2)
                wvp, wkp = ws.get((c * 3 + br) * 2 + 1)
                for G in range(2):
                    def ev_gl(o, ok, br=br):
                        act(sig[br], o, AF.Sigmoid, [ok], [("sig", br)])
                    proj(wvg, wkg, 16, lambda kc, G=G: hT(kc, G + 2), ev_gl)

                    def ev_p(o, ok, br=br, G=G, c=c):
                        if br == 0:
                            tt(yacc0[G], o, sig[0], ALU.mult, [ok, ("sig", 0)], [("yacc0", G)])
                        elif br == 1:
                            tt(ytmp[0], o, sig[1], ALU.mult, [ok, ("sig", 1)], [("ytmp", 0)])
                            tt(yacc1[G], yacc0[G], ytmp[0], ALU.add, [("yacc0", G), ("ytmp", 0)], [("yacc1", G)])
                        else:
                            tt(ytmp[1], o, sig[2], ALU.mult, [ok, ("sig", 2)], [("ytmp", 1)])
                            tt(yT[:, c, G * 512:(G + 1) * 512], yacc1[G], ytmp[1], ALU.add,
                               [("yacc1", G), ("ytmp", 1)], [("yT", c, G)])
                    proj(wvp, wkp, gk, lambda kc, G=G, gten=gten, gname=gname: (gten[:, kc, G * 512:(G + 1) * 512], (gname, kc, G)), ev_p)

        sch.barrier()
        if DUMP:
            dma("sp", dgA, vb(64, 8192), "dump0", (), [("dump", 0)])
            dma("sp", dgB, vb(80, 8192), "dump1", (), [("dump", 1)])
            dma("sp", dgM, vb(96, 4096), "dump2", (), [("dump", 2)])
            dma("sp", dyT, vb(TB + 8, 16384), "dump3", (), [("dump", 3)])
            sch.barrier()
        if STOP < 8:
            sch.emit()
            return nc, sch
        return_stage_e(nc, sch, vb, vf, banks, yT, WOUT, gfin, xo, out, act, tt, stt, recip, mm, dma, PSB, wo_slots)
        sch.barrier()
        sch.emit()
    return nc, sch


def return_stage_e(nc, sch, vb, vf, banks, yT, WOUT, gfin, xo, out, act, tt, stt, recip, mm, dma, PSB, wo_slots):
    xr = vf(80, 8 * 2048).rearrange("p (t n) -> p t n", t=8)
    gfb = vf(0, 2048)
    ot = [vf(8 + 8 * i, 2048) for i in range(2)]
    sqj = vb(24, 2048)
    ssum = vf(28, 1)
    rs1 = vf(28.25, 1)
    rs2 = vf(28.5, 1)
    dma("sp", gfb, gfin.partition_broadcast(128), "c_gf", (), [("gfb",)])
    for t in range(8):
        dma("sp", xr[:, t, :], xo[t * 128:(t + 1) * 128, :], f"xr{t}", (), [("xr", t, g) for g in range(4)])
    nb = 0
    for g in range(4):
        wv = wo_slots[g % 2]
        if g >= 1:
            wo_load(dma, wv, WOUT, g)
        if g + 1 < 4 and g + 1 >= 2:
            pass
        for t in range(8):
            b = nb % 4
            nb += 1
            for kc in range(16):
                mm(banks[b][:, :], yT[:, kc, t * 128:(t + 1) * 128], wv[:, kc, :],
                   kc == 0, kc == 15, [("yT", kc, t // 4), ("wo", g % 2)], [PSB(b)])
            tt(xr[:, t, g * 512:(g + 1) * 512], banks[b][:, :], xr[:, t, g * 512:(g + 1) * 512], ALU.add,
               [PSB(b), ("xr", t, g)], [("xr", t, g)])
            if g == 3:
                p = t % 2
                rk = [("xr", t, gg) for gg in range(4)]
                act(sqj, xr[:, t, :], AF.Square, rk, [("sqj",), ("ssum",)], accum_out=ssum)
                sch.add("dve", lambda: nc.vector.tensor_scalar(out=rs1, in0=ssum, scalar1=1.0 / D, scalar2=EPS, op0=ALU.mult, op1=ALU.add),
                        [("ssum",)], [("rs1",)])
                act(rs1, rs1, AF.Sqrt, [("rs1",)], [("rs1",)])
                recip(rs2, rs1, [("rs1",)], [("rs2",)])
                stt(ot[p], xr[:, t, :], rs2, gfb, ALU.mult, ALU.mult, rk + [("rs2",), ("gfb",)], [("ot", p)])
                dma("sp", out[t * 128:(t + 1) * 128, :], ot[p], f"ot{p}", [("ot", p)], [("out", t)])


def wo_load(dma, wv, WOUT, g):
    for q in range(4):
        dma("pool", wv[:, 4 * q:4 * q + 4, :], WOUT[:, 4 * q:4 * q + 4, g * 512:(g + 1) * 512],
            f"wo{g % 2}_{q}", (), [("wo", g % 2)])


def _t5_bucket(n):
    n = np.maximum(n, 0)
    max_exact = 16
    nf = np.maximum(n, 1).astype(np.float32)
    large = max_exact + (np.log(nf / max_exact) / math.log(128 / max_exact) * (32 - max_exact)).astype(np.int32)
    large = np.minimum(large, 31)
    return np.where(n < max_exact, n, large)


def _chunked(w, cols_list, kc):
    cols = np.concatenate(cols_list)
    n = len(cols_list)
    a = w[:, cols].reshape(kc, 128, n, 128).transpose(2, 1, 0, 3)
    return np.ascontiguousarray(a, dtype=np.float32)


_CACHE = {}


def kernel(x, mem, g_norm, w_in, g_cq, w_uq, g_ckv, w_ukv, g_mem, w_mem_kv, rel_bias,
           w_p_moba, w_p_mla, w_p_mem, w_out, g_final):
    f32 = np.float32
    x = np.asarray(x, f32)
    mem = np.asarray(mem, f32)
    w_in0 = np.asarray(w_in, f32)[0]
    ar = np.arange
    cl = []
    for h in range(8):
        cl += [1024 + h * 128 + ar(128), 2048 + h * 128 + ar(128), h * 128 + ar(128), 3072 + h * 128 + ar(128)]
    cl += [4608 + ar(128), 4608 + 128 + ar(128)]
    cl += [np.concatenate([4864 + ar(64), 4864 + 32 + ar(32), 4864 + ar(32)])]
    cl += [4096 + c * 128 + ar(128) for c in range(4)]
    cl += [4928 + h * 128 + ar(128) for h in range(8)]
    cl += [5952 + h * 128 + ar(128) for h in range(4)]
    cl += [6464 + h * 128 + ar(128) for h in range(4)]
    for br in range(3):
        cl += [6976 + br * 2048 + c * 128 + ar(128) for c in range(16)]
    assert len(cl) == N_WIN
    WIN = _chunked(w_in0, cl, 16)
    cl = []
    for h in range(8):
        cl += [h * 192 + ar(128), np.concatenate([h * 192 + 128 + ar(64), h * 192 + 128 + 32 + ar(32), h * 192 + 128 + ar(32)])]
    WUQ = _chunked(np.asarray(w_uq, f32)[0], cl, 4)
    cl = []
    for h in range(8):
        cl += [h * 256 + ar(128), h * 256 + 128 + ar(128)]
    WUKV = _chunked(np.asarray(w_ukv, f32)[0], cl, 2)
    WMKV = _chunked(np.asarray(w_mem_kv, f32)[0], [c * 128 + ar(128) for c in range(8)], 16)
    WPA = _chunked(np.asarray(w_p_moba, f32)[0], [c * 128 + ar(128) for c in range(16)], 8)
    WPB = _chunked(np.asarray(w_p_mla, f32)[0], [c * 128 + ar(128) for c in range(16)], 8)
    WPM = _chunked(np.asarray(w_p_mem, f32)[0], [c * 128 + ar(128) for c in range(16)], 4)
    WOUT = np.ascontiguousarray(np.asarray(w_out, f32)[0].reshape(16, 128, 2048).transpose(1, 0, 2))
    gcols = np.zeros((128, 40), f32)
    gcols[:, 0:16] = np.asarray(g_norm, f32)[0].reshape(16, 128).T
    gcols[:, 16:32] = np.asarray(g_mem, f32)[0].reshape(16, 128).T
    gcols[:, 32:36] = np.asarray(g_cq, f32)[0].reshape(4, 128).T
    gcols[:, 36:38] = np.asarray(g_ckv, f32)[0].reshape(2, 128).T
    gfin = np.asarray(g_final, f32).reshape(1, D)
    half = 32
    inv = (10000.0 ** (-np.arange(half, dtype=f32) / half)).astype(f32)
    i64 = np.arange(64)
    dd = np.arange(384) - 127
    E1 = np.zeros((33, 384), f32)
    bk = _t5_bucket(dd)
    for i in range(383):
        if dd[i] >= 0:
            E1[bk[i], i] += 1.0
            E1[31, i] -= 1.0
        else:
            E1[32, i] = 1.0
    rbaug = np.concatenate([np.asarray(rel_bias, f32), np.full((1, 8), NEG, f32)], axis=0)
    identb = np.eye(128, dtype=f32).astype(ml_dtypes.bfloat16)
    SELc = np.zeros((8, 8, 128), f32)
    for n in range(8):
        SELc[n, n, :] = 1.0
    SELc = SELc.reshape(8, 1024).astype(ml_dtypes.bfloat16)
    kk = np.arange(128)[:, None]
    qq = np.arange(128)[None, :]
    CAUSc = np.where(qq < kk, NEG, 0.0).astype(f32).astype(ml_dtypes.bfloat16)

    in_maps = []
    for c in range(NCORE):
        b, hf = c // 2, c % 2
        own = x[b, hf * 1024:(hf + 1) * 1024]
        ctx = x[b, 0:1024]
        xT = np.ascontiguousarray(np.concatenate([ctx, own], axis=0).T)
        pos = np.concatenate([np.arange(1024) + (hf - 1) * 1024, np.arange(1024) + hf * 1024]).astype(f32)
        ang = pos[None, :] * inv[i64 % 32][:, None]
        cosT = np.cos(ang).astype(f32)
        sn = np.sin(ang).astype(f32)
        ssinT = np.where((i64 < 32)[:, None], -sn, sn).astype(f32)
        pastb = np.zeros((128, 8, 8), f32)
        pasti = np.zeros((128, 8, 8), f32)
        npsel = np.zeros((128, 8, 8), f32)
        for i in range(8):
            ownblk = 4 + i // 2
            for n in range(8):
                past = (n < 4 and hf == 1) or (4 <= n < ownblk)
                pastb[:, i, n] = 0.0 if past else -1e30
                pasti[:, i, n] = 1.0 if past else 0.0
                npsel[:, i, n] = 0.0 if (past or n == ownblk) else NEG
        ctxb = np.full((128, 1), 0.0 if hf == 1 else NEG, f32)
        in_maps.append(dict(
            xT=xT, xo=np.ascontiguousarray(own), memT=np.ascontiguousarray(mem[b].T),
            WIN=WIN, WUQ=WUQ, WUKV=WUKV, WMKV=WMKV, WPA=WPA, WPB=WPB, WPM=WPM, WOUT=WOUT,
            gcols=gcols, gfin=gfin, cosT=cosT, ssinT=ssinT,
            pastb=pastb.reshape(128, 64), pasti=pasti.reshape(128, 64), npsel=npsel.reshape(128, 64),
            ctxb=ctxb, E1=E1, rbaug=rbaug, identb=identb, SELc=SELc, CAUSc=CAUSc))

    if "nc" not in _CACHE:
        _CACHE["nc"] = build_program()
    nc, _ = _CACHE["nc"]
    res = run_bass_kernel_spmd(nc, in_maps, core_ids=list(range(NCORE)))
    if DUMP:
        _CACHE["dump"] = [{k: np.asarray(res.results[c][k]) for k in ("dgA", "dgB", "dgM", "dyT", "dL", "dH")} for c in range(NCORE)]
    outp = np.zeros((B, S, D), f32)
    for c in range(NCORE):
        b, hf = c // 2, c % 2
        outp[b, hf * 1024:(hf + 1) * 1024] = res.results[c]["out"]
    return outp
```
